# Optimizing a Trainium2 kernel written in Bass

```python
import jax
import jax.numpy as jnp
from jax import lax
import numpy as np

D_MODEL = 1024
BATCH = 2
SEQ = 8192
DEPTH = 4

GRID_W = 64
CTX_LEN = 256
CHUNK = 64
CONV_K = 7
EPS = 1e-6
F32 = jnp.float32

M_HEADS = 16
M_HEAD_DIM = 64
M_INNER = M_HEADS * M_HEAD_DIM
M_GROUPS = 2
M_STATE = 128
M_CONV_DIM = M_INNER + 2 * M_GROUPS * M_STATE
M_IN = M_INNER + M_CONV_DIM + 2 * M_HEADS

R_HEADS = 16
R_HEAD_DIM = 64
R_W = R_HEADS * R_HEAD_DIM
R_DECAY_LORA = 64
R_AAA_LORA = 64
R_GATE_LORA = 160
R_IN = 3 * R_W + 2 * R_DECAY_LORA + 2 * R_AAA_LORA + R_GATE_LORA
R_GN_EPS = 64e-5

G_HEADS = 8
G_HEAD_DIM = 128
G_W = G_HEADS * G_HEAD_DIM
G_IN = 4 * G_W + 4 * G_HEADS

R_OFF = M_IN
G_OFF = R_OFF + R_IN
GATE_OFF = G_OFF + G_IN
N_IN = GATE_OFF + 3 * D_MODEL

D_FF = 4 * D_MODEL

kernel_name = 'hybrid_ssd_rwkv7_gdn_prefix_dit'


def split_cols(a, sizes):
    return jnp.split(a, np.cumsum(sizes)[:-1].tolist(), axis=-1)


def rms_norm(x, g):
    x32 = x.astype(F32)
    y = x32 * lax.rsqrt(jnp.mean(x32 * x32, axis=-1, keepdims=True) + EPS)
    return (y * g.astype(F32)).astype(x.dtype)


def l2norm(x):
    return x * lax.rsqrt(jnp.sum(x * x, axis=-1, keepdims=True) + EPS)


def modulate(x, g, shift, scale):
    return rms_norm(x, g) * (1 + scale) + shift


def ada_params(cond, w, b):
    m = jax.nn.silu(cond) @ w + b
    return split_cols(m[..., None, :], [D_MODEL] * 6)


def conv_centred(x, w):
    k = w.shape[0]
    return lax.conv_general_dilated(x, w[:, None, :].astype(x.dtype), window_strides=(1,),
                                    padding=[(k // 2, k // 2)], dimension_numbers=('NWC', 'WIO', 'NWC'),
                                    feature_group_count=x.shape[-1])


def to_scan_order(h, layer):
    if layer % 2 == 0:
        return h
    b, t, d = h.shape
    rows = t // GRID_W
    return h.reshape(b, rows, GRID_W, d).transpose(0, 2, 1, 3).reshape(b, t, d)


def from_scan_order(h, layer):
    if layer % 2 == 0:
        return h
    b, t, d = h.shape
    rows = t // GRID_W
    return h.reshape(b, GRID_W, rows, d).transpose(0, 2, 1, 3).reshape(b, t, d)


def prefix_bidirectional(scan_fn, ctx_dirs, lat_dirs, h0):
    flip = lambda arrs: [jnp.flip(a, axis=1) for a in arrs]
    yc_f, hc_f = scan_fn(*ctx_dirs[0], h0)
    yl_f, _ = scan_fn(*lat_dirs[0], hc_f)
    yc_b, hc_b = scan_fn(*flip(ctx_dirs[1]), h0)
    yl_b, _ = scan_fn(*flip(lat_dirs[1]), hc_b)
    return yc_f + jnp.flip(yc_b, axis=1), yl_f + jnp.flip(yl_b, axis=1)


def ssd_chunked(xdt, la, bm, cm, h0):
    b, t, h, p = xdt.shape
    g, n = bm.shape[2], bm.shape[3]
    r = h // g
    nc = t // CHUNK
    xdt = xdt.reshape(b, nc, CHUNK, g, r, p)
    la = la.reshape(b, nc, CHUNK, g, r)
    bm = bm.reshape(b, nc, CHUNK, g, n)
    cm = cm.reshape(b, nc, CHUNK, g, n)
    cs = jnp.cumsum(la, axis=2)
    cs_t = jnp.moveaxis(cs, 2, -1)
    tri = jnp.tril(jnp.ones((CHUNK, CHUNK), bool))
    seg = jnp.exp(jnp.where(tri, cs_t[..., :, None] - cs_t[..., None, :], -jnp.inf))
    cb = jnp.einsum('bclgn,bcsgn->bcgls', cm, bm)
    y_diag = jnp.einsum('bcgrls,bcsgrp->bclgrp', cb[:, :, :, None] * seg, xdt)
    last = cs[:, :, -1]
    states = jnp.einsum('bclgn,bclgr,bclgrp->bcgrpn', bm, jnp.exp(last[:, :, None] - cs), xdt)

    def step(hc, inp):
        st, dec = inp
        return hc * dec[..., None, None] + st, hc

    h_T, h_start = lax.scan(step, h0, (jnp.moveaxis(states, 1, 0), jnp.moveaxis(jnp.exp(last), 1, 0)))
    h_start = jnp.moveaxis(h_start, 0, 1)
    y_off = jnp.einsum('bclgn,bcgrpn,bclgr->bclgrp', cm, h_start, jnp.exp(cs))
    return (y_diag + y_off).reshape(b, t, h, p), h_T


def mamba_prepare(p, conv_w, conv_b, a_log, dt_bias):
    b, t, _ = p.shape
    z, xbc, dt_raw = split_cols(p, [M_INNER, M_CONV_DIM, 2 * M_HEADS])
    xbc = jax.nn.silu(conv_centred(xbc, conv_w) + conv_b).astype(F32)
    xs, bm, cm = split_cols(xbc, [M_INNER, M_GROUPS * M_STATE, M_GROUPS * M_STATE])
    xs = xs.reshape(b, t, M_HEADS, M_HEAD_DIM)
    bm = bm.reshape(b, t, M_GROUPS, M_STATE)
    cm = cm.reshape(b, t, M_GROUPS, M_STATE)
    dt = jax.nn.softplus(dt_raw.astype(F32).reshape(b, t, 2, M_HEADS) + dt_bias)
    a = -jnp.exp(a_log.astype(F32))
    dirs = tuple((xs * dt[:, :, d, :, None], dt[:, :, d] * a[d], bm, cm) for d in range(2))
    return xs, z, dirs


def mamba_mixer(pc, pl, conv_w, conv_b, a_log, dt_bias, d_skip, norm_g):
    xs_c, z_c, dirs_c = mamba_prepare(pc, conv_w, conv_b, a_log, dt_bias)
    xs_l, z_l, dirs_l = mamba_prepare(pl, conv_w, conv_b, a_log, dt_bias)
    h0 = jnp.zeros((pc.shape[0], M_GROUPS, M_HEADS // M_GROUPS, M_HEAD_DIM, M_STATE), F32)
    y_c, y_l = prefix_bidirectional(ssd_chunked, dirs_c, dirs_l, h0)

    def finish(y, xs, z):
        b, t = y.shape[:2]
        y = (y + d_skip.astype(F32)[:, None] * xs).reshape(b, t, M_INNER) * jax.nn.silu(z.astype(F32))
        y = rms_norm(y.reshape(b, t, M_GROUPS, M_INNER // M_GROUPS), norm_g.reshape(M_GROUPS, -1))
        return y.reshape(b, t, M_INNER)

    return finish(y_c, xs_c, z_c), finish(y_l, xs_l, z_l)


def rwkv7_scan(r, w, k, v, kk, a, s0):
    def step(s, inp):
        r_t, w_t, k_t, v_t, kk_t, a_t = inp
        sa = jnp.einsum('bhvk,bhk->bhv', s, -kk_t)
        s = s * w_t[:, :, None, :] + sa[..., None] * (kk_t * a_t)[:, :, None, :] + v_t[..., None] * k_t[:, :, None, :]
        return s, jnp.einsum('bhvk,bhk->bhv', s, r_t)

    s_T, y = lax.scan(step, s0, tuple(jnp.moveaxis(u, 1, 0) for u in (r, w, k, v, kk, a)))
    return jnp.moveaxis(y, 0, 1), s_T


def rwkv_prepare(u, mu, w0, w2, a0, a2, g2, k_k, k_a, r_k):
    b, t, _ = u.shape
    u = u.astype(F32)
    zero = jnp.zeros_like(u[:, :1])
    u_nb = 0.5 * (jnp.concatenate([zero, u[:, :-1]], axis=1) + jnp.concatenate([u[:, 1:], zero], axis=1))
    u = u + mu * (u_nb - u)
    r, k, v, wl, al, gl = split_cols(u, [R_W, R_W, R_W, 2 * R_DECAY_LORA, 2 * R_AAA_LORA, R_GATE_LORA])
    wl = jnp.tanh(wl.reshape(b, t, 2, R_DECAY_LORA))
    logw = -jax.nn.softplus(-(w0 + jnp.einsum('btdr,drc->btdc', wl, w2))) - 0.5
    w = jnp.exp(-jnp.exp(logw))
    a = jax.nn.sigmoid(a0 + jnp.einsum('btdr,drc->btdc', al.reshape(b, t, 2, R_AAA_LORA), a2))
    gate = jax.nn.sigmoid(gl) @ g2
    heads = lambda z: z.reshape(*z.shape[:-1], R_HEADS, R_HEAD_DIM)
    kk = l2norm(heads(k * k_k))
    kd = heads(k[:, :, None] * (1 + (a - 1) * k_a))
    r, v, w, a = heads(r), heads(v), heads(w), heads(a)
    bonus = jnp.sum(r * kd.sum(axis=2) * r_k, axis=-1, keepdims=True) * v
    dirs = tuple((r, w[:, :, d], kd[:, :, d], v, kk, a[:, :, d]) for d in range(2))
    return dirs, bonus, gate


def rwkv_mixer(pc, pl, mu, w0, w2, a0, a2, g2, k_k, k_a, r_k, ln_g, ln_b):
    dirs_c, bonus_c, gate_c = rwkv_prepare(pc, mu, w0, w2, a0, a2, g2, k_k, k_a, r_k)
    dirs_l, bonus_l, gate_l = rwkv_prepare(pl, mu, w0, w2, a0, a2, g2, k_k, k_a, r_k)
    s0 = jnp.zeros((pc.shape[0], R_HEADS, R_HEAD_DIM, R_HEAD_DIM), F32)
    y_c, y_l = prefix_bidirectional(rwkv7_scan, dirs_c, dirs_l, s0)

    def finish(y, bonus, gate):
        mean = jnp.mean(y, axis=-1, keepdims=True)
        var = jnp.mean(jnp.square(y - mean), axis=-1, keepdims=True)
        y = (y - mean) * lax.rsqrt(var + R_GN_EPS) * ln_g.reshape(R_HEADS, R_HEAD_DIM) + ln_b.reshape(R_HEADS, R_HEAD_DIM)
        return (y + bonus).reshape(*y.shape[:2], R_W) * gate

    return finish(y_c, bonus_c, gate_c), finish(y_l, bonus_l, gate_l)


def gdn_chunked(q, k, v, g, beta, s0):
    b, t, h, _ = q.shape
    dv = v.shape[-1]
    nc = t // CHUNK
    chunks = lambda z: jnp.moveaxis(z.reshape(b, nc, CHUNK, h, *z.shape[3:]), 3, 2)
    q, k, v, g, beta = (chunks(z) for z in (q, k, v, g, beta))
    gcs = jnp.cumsum(g, axis=-1)
    tri = jnp.tril(jnp.ones((CHUNK, CHUNK), bool))
    strict = jnp.tril(jnp.ones((CHUNK, CHUNK), bool), -1)
    gam = jnp.exp(jnp.where(tri, gcs[..., :, None] - gcs[..., None, :], -jnp.inf))
    kb = k * beta[..., None]
    eye = jnp.eye(CHUNK, dtype=F32)
    lhs = jnp.where(strict, jnp.einsum('bchlk,bchsk->bchls', kb, k) * gam, 0.0) + eye
    rhs = jnp.concatenate([v * beta[..., None], kb * jnp.exp(gcs)[..., None]], axis=-1)
    sol = lax.linalg.triangular_solve(lhs, rhs, left_side=True, lower=True, unit_diagonal=True)
    u, wk = sol[..., :dv], sol[..., dv:]
    aqk = jnp.einsum('bchlk,bchsk->bchls', q, k) * gam
    glast = gcs[..., -1:]
    qe = q * jnp.exp(gcs)[..., None]
    ke = k * jnp.exp(glast - gcs)[..., None]
    dl = jnp.exp(glast[..., 0])

    def step(s, inp):
        u_c, wk_c, qe_c, ke_c, aqk_c, dl_c = inp
        v_new = u_c - jnp.einsum('bhlk,bhkv->bhlv', wk_c, s)
        o = jnp.einsum('bhlk,bhkv->bhlv', qe_c, s) + jnp.einsum('bhls,bhsv->bhlv', aqk_c, v_new)
        s = s * dl_c[..., None, None] + jnp.einsum('bhlk,bhlv->bhkv', ke_c, v_new)
        return s, o

    s_T, o = lax.scan(step, s0, tuple(jnp.moveaxis(z, 1, 0) for z in (u, wk, qe, ke, aqk, dl)))
    o = jnp.moveaxis(jnp.moveaxis(o, 0, 1), 2, 3)
    return o.reshape(b, t, h, dv), s_T


def gdn_prepare(p, conv_w, a_log, dt_bias):
    b, t, _ = p.shape
    qkv, z, beta_raw, alpha_raw = split_cols(p, [3 * G_W, G_W, 2 * G_HEADS, 2 * G_HEADS])
    qkv = jax.nn.silu(conv_centred(qkv, conv_w)).astype(F32)
    q, k, v = (s.reshape(b, t, G_HEADS, G_HEAD_DIM) for s in split_cols(qkv, [G_W] * 3))
    q = l2norm(q) * (G_HEAD_DIM ** -0.5)
    k = l2norm(k)
    beta = jax.nn.sigmoid(beta_raw.astype(F32).reshape(b, t, 2, G_HEADS))
    g = -jnp.exp(a_log.astype(F32)) * jax.nn.softplus(alpha_raw.astype(F32).reshape(b, t, 2, G_HEADS) + dt_bias)
    dirs = tuple((q, k, v, g[:, :, d], beta[:, :, d]) for d in range(2))
    return dirs, z


def gdn_mixer(pc, pl, conv_w, a_log, dt_bias, norm_g):
    dirs_c, z_c = gdn_prepare(pc, conv_w, a_log, dt_bias)
    dirs_l, z_l = gdn_prepare(pl, conv_w, a_log, dt_bias)
    s0 = jnp.zeros((pc.shape[0], G_HEADS, G_HEAD_DIM, G_HEAD_DIM), F32)
    y_c, y_l = prefix_bidirectional(gdn_chunked, dirs_c, dirs_l, s0)

    def finish(y, z):
        b, t = y.shape[:2]
        y = rms_norm(y, norm_g) * jax.nn.silu(z.astype(F32).reshape(b, t, G_HEADS, G_HEAD_DIM))
        return y.reshape(b, t, G_W)

    return finish(y_c, z_c), finish(y_l, z_l)


def merge_branches(p, y_m, y_r, y_g, w_bm, w_br, w_bg, w_out):
    g_m, g_r, g_g = split_cols(jax.nn.sigmoid(p[..., GATE_OFF:]), [D_MODEL] * 3)
    dt = p.dtype
    mixed = g_m * (y_m.astype(dt) @ w_bm) + g_r * (y_r.astype(dt) @ w_br) + g_g * (y_g.astype(dt) @ w_bg)
    return mixed @ w_out


def sq_relu_mlp(h, w1, w2):
    return jnp.square(jax.nn.relu(h @ w1)) @ w2


def setup_inputs(seed: int = 0) -> dict:
    key = jax.random.key(seed)
    ks = iter(jax.random.split(key, 48))
    nrm = lambda shape, s: jax.random.normal(next(ks), shape, F32) * s
    unif = lambda shape, lo, hi: jax.random.uniform(next(ks), shape, F32, lo, hi)
    L = DEPTH
    return {
        'x': nrm((BATCH, SEQ, D_MODEL), 1.0),
        'c': nrm((BATCH, D_MODEL), 1.0),
        'ctx': nrm((BATCH, CTX_LEN, D_MODEL), 1.0),
        'c_ctx': nrm((D_MODEL,), 1.0),
        'ada_w': nrm((L, D_MODEL, 6 * D_MODEL), 0.5 * D_MODEL ** -0.5),
        'ada_b': nrm((L, 6 * D_MODEL), 0.02),
        'norm1_g': 1.0 + nrm((L, D_MODEL), 0.02),
        'norm2_g': 1.0 + nrm((L, D_MODEL), 0.02),
        'final_g': 1.0 + nrm((D_MODEL,), 0.02),
        'w_in': nrm((L, D_MODEL, N_IN), D_MODEL ** -0.5),
        'm_conv_w': nrm((L, CONV_K, M_CONV_DIM), CONV_K ** -0.5),
        'm_conv_b': nrm((L, M_CONV_DIM), 0.02),
        'm_a_log': jnp.log(unif((L, 2, M_HEADS), 1.0, 16.0)),
        'm_dt_bias': unif((L, 2, M_HEADS), -4.6, -2.3),
        'm_d': 1.0 + nrm((L, M_HEADS), 0.1),
        'm_norm_g': 1.0 + nrm((L, M_INNER), 0.02),
        'r_mu': unif((L, R_IN), 0.0, 1.0),
        'r_w0': unif((L, 2, R_W), -6.0, -1.0),
        'r_w2': nrm((L, 2, R_DECAY_LORA, R_W), 0.1 * R_DECAY_LORA ** -0.5),
        'r_a0': nrm((L, 2, R_W), 0.1),
        'r_a2': nrm((L, 2, R_AAA_LORA, R_W), 0.1 * R_AAA_LORA ** -0.5),
        'r_g2': nrm((L, R_GATE_LORA, R_W), R_GATE_LORA ** -0.5),
        'r_k_k': 0.85 + nrm((L, R_W), 0.02),
        'r_k_a': 1.0 + nrm((L, R_W), 0.02),
        'r_r_k': nrm((L, R_HEADS, R_HEAD_DIM), 0.1),
        'r_ln_g': 1.0 + nrm((L, R_W), 0.02),
        'r_ln_b': nrm((L, R_W), 0.02),
        'g_conv_w': nrm((L, CONV_K, 3 * G_W), CONV_K ** -0.5),
        'g_a_log': jnp.log(unif((L, 2, G_HEADS), 1.0, 16.0)),
        'g_dt_bias': unif((L, 2, G_HEADS), -4.6, -2.3),
        'g_norm_g': 1.0 + nrm((L, G_HEAD_DIM), 0.02),
        'w_bm': nrm((L, M_INNER, D_MODEL), M_INNER ** -0.5),
        'w_br': nrm((L, R_W, D_MODEL), R_W ** -0.5),
        'w_bg': nrm((L, G_W, D_MODEL), G_W ** -0.5),
        'w_out': nrm((L, D_MODEL, D_MODEL), D_MODEL ** -0.5),
        'w_ff1': nrm((L, D_MODEL, D_FF), D_MODEL ** -0.5),
        'w_ff2': nrm((L, D_FF, D_MODEL), D_FF ** -0.5),
    }


def reference(x, c, ctx, c_ctx, ada_w, ada_b, norm1_g, norm2_g, final_g, w_in,
              m_conv_w, m_conv_b, m_a_log, m_dt_bias, m_d, m_norm_g,
              r_mu, r_w0, r_w2, r_a0, r_a2, r_g2, r_k_k, r_k_a, r_r_k, r_ln_g, r_ln_b,
              g_conv_w, g_a_log, g_dt_bias, g_norm_g,
              w_bm, w_br, w_bg, w_out, w_ff1, w_ff2):
    xl, xc = x, ctx
    for i in range(DEPTH):
        sh1_l, sc1_l, gt1_l, sh2_l, sc2_l, gt2_l = ada_params(c, ada_w[i], ada_b[i])
        sh1_c, sc1_c, gt1_c, sh2_c, sc2_c, gt2_c = ada_params(c_ctx, ada_w[i], ada_b[i])
        hl = to_scan_order(modulate(xl, norm1_g[i], sh1_l, sc1_l), i)
        hc = modulate(xc, norm1_g[i], sh1_c, sc1_c)
        pl = hl @ w_in[i]
        pc = hc @ w_in[i]
        ym_c, ym_l = mamba_mixer(pc[..., :R_OFF], pl[..., :R_OFF], m_conv_w[i], m_conv_b[i],
                                 m_a_log[i], m_dt_bias[i], m_d[i], m_norm_g[i])
        yr_c, yr_l = rwkv_mixer(pc[..., R_OFF:G_OFF], pl[..., R_OFF:G_OFF], r_mu[i], r_w0[i], r_w2[i],
                                r_a0[i], r_a2[i], r_g2[i], r_k_k[i], r_k_a[i], r_r_k[i], r_ln_g[i], r_ln_b[i])
        yg_c, yg_l = gdn_mixer(pc[..., G_OFF:GATE_OFF], pl[..., G_OFF:GATE_OFF], g_conv_w[i],
                               g_a_log[i], g_dt_bias[i], g_norm_g[i])
        out_l = from_scan_order(merge_branches(pl, ym_l, yr_l, yg_l, w_bm[i], w_br[i], w_bg[i], w_out[i]), i)
        xl = xl + gt1_l * out_l
        xl = xl + gt2_l * sq_relu_mlp(modulate(xl, norm2_g[i], sh2_l, sc2_l), w_ff1[i], w_ff2[i])
        if i < DEPTH - 1:
            xc = xc + gt1_c * merge_branches(pc, ym_c, yr_c, yg_c, w_bm[i], w_br[i], w_bg[i], w_out[i])
            xc = xc + gt2_c * sq_relu_mlp(modulate(xc, norm2_g[i], sh2_c, sc2_c), w_ff1[i], w_ff2[i])
    return rms_norm(xl, final_g)
```

```python
import contextlib
import numpy as np
import concourse.bass as bass
import concourse.mybir as mybir
from concourse.bass_utils import run_bass_kernel_spmd

F32 = mybir.dt.float32
BF16 = mybir.dt.bfloat16
AF = mybir.ActivationFunctionType
ALU = mybir.AluOpType
AX = mybir.AxisListType


class Buf:
    __slots__ = ("name", "ws", "rs", "sem")

    def __init__(self, name=""):
        self.name = name
        self.ws = []
        self.rs = []
        self.sem = None


class Op:
    __slots__ = ("eng", "fn", "deps", "sig", "dma", "semkey", "val", "gi", "barrier")


class Prog:
    ENGS = ("pe", "act", "dve", "pool", "sp")

    def __init__(self, nc):
        self.nc = nc
        self.ops = []
        self.last = {e: None for e in self.ENGS}
        self.dma_since_barrier = []
        self.bufslot = {}
        self.nslots = 0

    def add(self, eng, fn, reads=(), writes=(), dma=False, join=False, sem_buf=None):
        def flat(bs):
            o = []
            for b in bs:
                if isinstance(b, tuple):
                    o.extend(b)
                else:
                    o.append(b)
            return o
        reads = flat(reads)
        writes = flat(writes)
        op = Op()
        op.eng, op.fn, op.dma, op.sig, op.barrier = eng, fn, dma, False, False
        op.semkey = None
        op.val = 0
        deps = []
        for b in reads:
            deps.extend(b.ws)
        for b in writes:
            deps.extend(b.rs)
            if not join:
                deps.extend(b.ws)
        rawset = set()
        for b in reads:
            rawset.update(id(w) for w in b.ws)
        fd = []
        seen = set()
        for d in deps:
            if d is op or id(d) in seen:
                continue
            seen.add(id(d))
            if (not d.dma) and (not dma) and d.eng == eng and (id(d) not in rawset or eng == "pe"):
                continue
            fd.append(d)
        op.deps = fd
        for d in fd:
            d.sig = True
        for b in reads:
            b.rs.append(op)
        for b in writes:
            if join:
                b.ws.append(op)
            else:
                b.ws = [op]
                b.rs = []
        if dma:
            op.sig = True
            k = id(sem_buf if sem_buf is not None else writes[0])
            if k not in self.bufslot:
                self.bufslot[k] = len(self.bufslot)
                self.nslots = max(self.nslots, len(self.bufslot))
            op.semkey = self.bufslot[k]
            self.dma_since_barrier.append(op)
        op.gi = len(self.ops)
        self.ops.append(op)
        self.last[eng] = op
        return op

    def barrier(self, bufs=()):
        lasts = [self.last[e] for e in self.ENGS if self.last[e] is not None]
        dmas = list(self.dma_since_barrier)
        self.dma_since_barrier = []
        self.bufslot = {}
        for e in self.ENGS:
            op = Op()
            op.eng, op.fn, op.dma, op.sig, op.barrier = e, None, False, False, True
            op.semkey = None
            op.val = 0
            op.deps = [d for d in lasts if d.dma or d.eng != e] + dmas
            for d in op.deps:
                d.sig = True
            op.gi = len(self.ops)
            self.ops.append(op)
            self.last[e] = op
        for b in bufs:
            b.ws = []
            b.rs = []

    def emit(self):
        nc = self.nc
        with contextlib.ExitStack() as st:
            esem = {e: st.enter_context(nc.semaphore("s_" + e)) for e in self.ENGS}
            ecnt = {e: 0 for e in self.ENGS}
            dsem = {}
            dcnt = {}
            for op in self.ops:
                if op.barrier or not op.sig:
                    continue
                if op.dma:
                    k = op.semkey
                    if k not in dsem:
                        dsem[k] = st.enter_context(nc.semaphore("d%d" % len(dsem)))
                        dcnt[k] = 0
                    dcnt[k] += 16
                    op.val = dcnt[k]
                else:
                    ecnt[op.eng] += 1
                    op.val = ecnt[op.eng]
            self.nsem = len(dsem) + 5
            byeng = {e: [o for o in self.ops if o.eng == e] for e in self.ENGS}

            def run(ename, eng):
                known = {}
                for op in byeng[ename]:
                    for d in op.deps:
                        sem = dsem[d.semkey] if d.dma else esem[d.eng]
                        key = d.semkey if d.dma else d.eng
                        if known.get(key, 0) >= d.val:
                            continue
                        known[key] = d.val
                        eng.wait_ge(sem, d.val)
                    if op.barrier:
                        continue
                    inst = op.fn(eng)
                    if op.sig:
                        if op.dma:
                            inst.then_inc(dsem[op.semkey], 16)
                        else:
                            inst.then_inc(esem[ename], 1)

            with nc.Block() as block:
                @block.tensor
                def _(e):
                    run("pe", e)

                @block.scalar
                def _(e):
                    run("act", e)

                @block.vector
                def _(e):
                    run("dve", e)

                @block.gpsimd
                def _(e):
                    run("pool", e)

                @block.sync
                def _(e):
                    run("sp", e)


class V:
    __slots__ = ("ap", "b")

    def __init__(self, ap, b):
        self.ap = ap
        self.b = b

    def __getitem__(self, k):
        return V(self.ap[k], self.b)

    def unsq(self, a):
        return V(self.ap.unsqueeze(a), self.b)

    def bc(self, shape):
        return V(self.ap.to_broadcast(list(shape)), self.b)

    def rr(self, pat, **kw):
        return V(self.ap.rearrange(pat, **kw), self.b)

    def pb(self, n=128):
        return V(self.ap.partition_broadcast(n), self.b)

    @property
    def shape(self):
        return tuple(self.ap.shape)


def _dsize(dt):
    return 2 if dt == BF16 else 4


class Base:
    ARENA = 48000

    def __init__(self, debug_out=(), debug_in=()):
        self.nc = bass.Bass("TRN2", target_bir_lowering=False)
        self.P = Prog(self.nc)
        self.debug_out = set(debug_out)
        self.debug_in = set(debug_in)
        self.st = contextlib.ExitStack()
        self.PAGE = 4096
        sizes = [self.PAGE] * (self.ARENA // self.PAGE)
        if self.ARENA % self.PAGE:
            sizes.append(self.ARENA % self.PAGE)
        self.pages = [self.st.enter_context(self.nc.sbuf_tensor("pg%d" % j, [128, n], F32)) for j, n in enumerate(sizes)]
        self.pgsize = sizes
        self.pgoff = [0] * len(sizes)
        self.banks = [self.st.enter_context(self.nc.psum_tensor("bank%d" % j, [128, 512], F32)) for j in range(8)]
        self.bankbufs = [Buf("bank%d" % j) for j in range(8)]
        self.scr_bufs = []
        self.out_names = []

    def din(self, name, shape, dt=F32):
        return V(self.nc.dram_tensor(name, list(shape), dt, kind="ExternalInput").ap(), Buf(name))

    def dout(self, name, shape, dt=F32):
        self.out_names.append(name)
        return V(self.nc.dram_tensor(name, list(shape), dt, kind="ExternalOutput").ap(), Buf(name))

    def scr(self, name, shape, dt=F32):
        if name in self.debug_out:
            v = self.dout(name, shape, dt)
        elif name in self.debug_in:
            v = self.din(name, shape, dt)
        else:
            v = V(self.nc.dram_tensor(name, list(shape), dt).ap(), Buf(name))
        self.scr_bufs.append(v.b)
        return v

    def sb(self, shape, dt=F32, name=""):
        n = 1
        for s in shape[1:]:
            n *= s
        nbytes = n * _dsize(dt)
        words = (nbytes + 3) // 4
        words = (words + 7) // 8 * 8
        pg = None
        for j in range(len(self.pages)):
            if self.pgoff[j] + words <= self.pgsize[j]:
                pg = j
                break
        assert pg is not None, ("SBUF overflow", name, words, self.pgoff)
        ap = self.pages[pg][0:shape[0], self.pgoff[pg]:self.pgoff[pg] + words]
        self.pgoff[pg] += words
        if dt != F32:
            ap = ap.bitcast(dt)
        ap = ap[:, 0:n]
        if len(shape) == 3:
            ap = ap.rearrange("p (a b) -> p a b", a=shape[1])
        elif len(shape) == 4:
            ap = ap.rearrange("p (a b c) -> p a b c", a=shape[1], b=shape[2])
        return V(ap, Buf(name))

    def pbank(self, k, shape, dt=F32, nb=1, name="", off=0):
        assert nb == 1
        n = 1
        for s in shape[1:]:
            n *= s
        ap = self.banks[k][0:shape[0], off:512]
        if dt != F32:
            ap = ap.bitcast(dt)
        ap = ap[:, 0:n]
        if len(shape) == 3:
            ap = ap.rearrange("p (a b) -> p a b", a=shape[1])
        return V(ap, self.bankbufs[k])

    def stage_begin(self):
        self.pgoff = [0] * len(self.pages)

    def stage_end(self):
        self.P.barrier(self.scr_bufs)

    def dma(self, q, out, in_, join=False):
        o, i = out.ap, in_.ap
        src_sb = "SB" in type(i.tensor).__name__
        dst_sb = "SB" in type(o.tensor).__name__
        sem_buf = in_.b if (src_sb and not dst_sb) else out.b
        if isinstance(sem_buf, tuple):
            sem_buf = sem_buf[0]
        self.P.add(q, lambda e: e.dma_start(out=o, in_=i), [in_.b], [out.b], dma=True, join=join, sem_buf=sem_buf)

    def mm(self, out, lhsT, rhs, start=True, stop=True):
        o, l, r = out.ap, lhsT.ap, rhs.ap
        self.P.add("pe", lambda e: e.matmul(o, lhsT=l, rhs=r, start=start, stop=stop), [lhsT.b, rhs.b], [out.b])

    def tr(self, out, in_, ident):
        o, i, d = out.ap, in_.ap, ident.ap
        self.P.add("pe", lambda e: e.transpose(out=o, in_=i, identity=d), [in_.b, ident.b], [out.b])

    def act(self, out, in_, func, bias=None, scale=None, accum=None):
        reads = [in_.b]
        writes = [out.b]
        kw = {}
        if bias is not None:
            if isinstance(bias, V):
                reads.append(bias.b)
                kw["bias"] = bias.ap
            else:
                kw["bias"] = bias
        if scale is not None:
            if isinstance(scale, V):
                reads.append(scale.b)
                kw["scale"] = scale.ap
            else:
                kw["scale"] = scale
        if accum is not None:
            writes.append(accum.b)
            kw["accum_out"] = accum.ap
        o, i = out.ap, in_.ap
        self.P.add("act", lambda e: e.activation(out=o, in_=i, func=func, **kw), reads, writes)

    def tt(self, eng, out, in0, in1, op):
        o, a, b = out.ap, in0.ap, in1.ap
        self.P.add(eng, lambda e: e.tensor_tensor(out=o, in0=a, in1=b, op=op), [in0.b, in1.b], [out.b])

    def ts(self, eng, out, in0, s1, op0, s2=None, op1=None):
        reads = [in0.b]
        a1 = s1
        if isinstance(s1, V):
            reads.append(s1.b)
            a1 = s1.ap
        a2 = s2
        if isinstance(s2, V):
            reads.append(s2.b)
            a2 = s2.ap
        o, a = out.ap, in0.ap
        if op1 is None:
            self.P.add(eng, lambda e: e.tensor_scalar(out=o, in0=a, scalar1=a1, scalar2=None, op0=op0), reads, [out.b])
        else:
            self.P.add(eng, lambda e: e.tensor_scalar(out=o, in0=a, scalar1=a1, scalar2=a2, op0=op0, op1=op1), reads, [out.b])

    def stt(self, out, in0, scalar, in1, op0, op1):
        reads = [in0.b, in1.b]
        sc = scalar
        if isinstance(scalar, V):
            reads.append(scalar.b)
            sc = scalar.ap
        o, a, b = out.ap, in0.ap, in1.ap
        self.P.add("dve", lambda e: e.scalar_tensor_tensor(out=o, in0=a, scalar=sc, in1=b, op0=op0, op1=op1), reads, [out.b])

    def cp(self, eng, out, in_):
        o, i = out.ap, in_.ap
        if eng == "act":
            self.P.add("act", lambda e: e.copy(out=o, in_=i), [in_.b], [out.b])
        else:
            self.P.add(eng, lambda e: e.tensor_copy(out=o, in_=i), [in_.b], [out.b])

    def recip(self, out, in_):
        o, i = out.ap, in_.ap
        self.P.add("dve", lambda e: e.reciprocal(out=o, in_=i), [in_.b], [out.b])

    def red(self, out, in_, op=ALU.add):
        o, i = out.ap, in_.ap
        self.P.add("dve", lambda e: e.tensor_reduce(out=o, in_=i, axis=AX.X, op=op), [in_.b], [out.b])

    def memset(self, eng, out, val):
        o = out.ap
        self.P.add(eng, lambda e: e.memset(o, val), [], [out.b])

    def finish(self):
        self.P.barrier()
        self.P.emit()
        self.st.close()
        return self.nc


EPS = 1e-6
PT_R, PT_MZ, PT_MDT, PTA_W = 0, 3488, 4512, 4544
PT_GZ, PT_GBA, PT_GATE, PTB_W = 0, 1024, 1056, 4128
PT_W = PTA_W + PTB_W
QT_X, QT_B, QT_C, QT_GQ, QT_GK, QT_GV, QT_W = 0, 1024, 1280, 1536, 2560, 3584, 4608


class MK(Base):
    def __init__(self, GW=64, depth=4, stages=None, par0=0, **kw):
        super().__init__(**kw)
        self.par0 = par0
        self.GW = GW
        self.SEQ = 128 * GW
        self.CTX = 256
        self.T = self.SEQ + self.CTX
        self.NT = self.T // 128
        self.L = depth
        L, T, NT = depth, self.T, self.NT
        d = self.din
        self.xin = d("xin", [T, 1024])
        self.cc = d("cc", [128, 8, 2])
        self.ada_w = d("ada_w", [L, 1024, 6144])
        self.ada_b = d("ada_b", [L, 6144])
        self.n1g = d("norm1_g", [L, 1024])
        self.n2g = d("norm2_g", [L, 1024])
        self.fing = d("final_g", [1, 1024])
        self.wT = d("wT", [L, 1024, PT_W])
        self.wF = d("wF", [L, 1024, QT_W])
        self.cw = d("cw", [L, 128, 36, 7])
        self.cb = d("cb", [L, 128, 36])
        self.w_bm = d("w_bm", [L, 1024, 1024])
        self.w_br = d("w_br", [L, 1024, 1024])
        self.w_bg = d("w_bg", [L, 1024, 1024])
        self.w_out = d("w_out", [L, 1024, 1024])
        self.w_ff1 = d("w_ff1", [L, 1024, 4096])
        self.w_ff2 = d("w_ff2", [L, 4096, 1024])
        self.m_norm_g = d("m_norm_g", [L, 1024])
        self.ident = d("ident", [128, 128])
        self.cmask = d("cmask", [5, 128, 128])
        self.lmask = d("lmask", [2, 7, 128, 128])
        self.mprm = d("mprm", [L, 80])
        self.gprm = d("gprm", [L, 160])
        self.rrows = d("rrows", [L, 5, 1024])
        self.rmu = d("rmu", [L, 3488])
        self.rw2 = d("rw2", [L, 2, 65, 1024])
        self.ra2 = d("ra2", [L, 2, 65, 1024])
        self.rg2 = d("rg2", [L, 160, 1024])
        self.zrow = d("zrow", [1, 3488])
        s = self.scr
        self.XR = s("XR", [T, 1024])
        self.MODS = s("MODS", [2, 6144])
        self.HT = s("HT", [NT, 128, 8, 128])
        self.PTA = s("PTA", [T, PTA_W])
        self.PTB = s("PTB", [T, PTB_W])
        self.PF = s("PF", [QT_W, T])
        self.QT = s("QT", [T, QT_W])
        self.YM = s("YM", [T, 1024])
        self.YR = s("YR", [T, 1024])
        self.YG = s("YG", [T, 1024])
        self.Y0 = s("Y0", [T, 1024])
        self.RSr = s("RSr", [T, 1024], BF16)
        self.RSv = s("RSv", [T, 1024], BF16)
        self.RSkk = s("RSkk", [T, 1024])
        self.RSgate = s("RSgate", [T, 1024])
        self.RSbonus = s("RSbonus", [T, 1024])
        self.RSlw = [s("RSlw%d" % j, [T, 1024]) for j in range(2)]
        self.RSkd = [s("RSkd%d" % j, [T, 1024], BF16) for j in range(2)]
        self.RSka = [s("RSka%d" % j, [T, 1024], BF16) for j in range(2)]
        self.out = self.dout("out", [self.SEQ, 1024])

    def tile_rows(self, dr, i, layer, buf=None):
        if i < 2 or (layer + self.par0) % 2 == 0:
            ap = dr.ap[i * 128:(i + 1) * 128]
        else:
            j = i - 2
            ap = dr.ap[self.CTX + j:self.T:self.GW]
        return V(ap, buf if buf is not None else dr.b)

    def load_row_b(self, q, src_row_ap, n, name=""):
        t = self.sb([128, n], F32, name)
        self.dma(q, t, V(src_row_ap.partition_broadcast(128), Buf()))
        return t

    def mod_tile(self, which, seg, g_row=None):
        t = self.sb([128, 1024], F32)
        self.load_mod_into(t, which, seg, g_row)
        return t

    def load_mod_into(self, t, which, seg, g_row=None, gtile=None):
        self.dma("sp", t, V(self.MODS.ap[seg, which * 1024:(which + 1) * 1024].partition_broadcast(128), self.MODS.b))
        if gtile is not None:
            self.stt(t, t, 1.0, gtile, ALU.add, ALU.mult)

    def rms_rstd(self, x, n, ss, rs, eps=EPS, junk=None):
        self.act(junk, x, AF.Square, accum=ss)
        self.act(rs, ss, AF.Sqrt, scale=1.0 / n, bias=eps)
        self.recip(rs, rs)

    def stage_init(self):
        self.stage_begin()
        T = self.T
        step = 1024
        for r0 in range(0, T, step):
            r1 = min(T, r0 + step)
            self.dma("sp", V(self.XR.ap[r0:r1], self.XR.b), V(self.xin.ap[r0:r1], self.xin.b), join=True)
        self.stage_end()

    def stage_ada(self, l):
        self.stage_begin()
        cc = self.sb([128, 8, 2])
        self.dma("sp", cc, self.cc)
        scs = self.sb([128, 8, 2])
        self.act(scs, cc, AF.Silu)
        adabs = [self.sb([2, 3072]) for _ in range(2)]
        for hh in range(2):
            self.dma("sp", adabs[hh], V(self.ada_b.ap[l, hh * 3072:(hh + 1) * 3072].partition_broadcast(2), self.ada_b.b))
        mrows = [self.sb([2, 3072]) for _ in range(2)]
        wts = [self.sb([128, 8, 512]) for _ in range(2)]
        pms = [self.pbank(k, [2, 512]) for k in range(2)]
        wv = self.ada_w.ap[l].rearrange("(k p) n -> p k n", p=128)
        for nb in range(12):
            wt = wts[nb % 2]
            self.dma("sp", wt, V(wv[:, :, nb * 512:(nb + 1) * 512], self.ada_w.b))
            pm = pms[nb % 2]
            for k in range(8):
                self.mm(pm, scs[:, k, :], wt[:, k, :], start=(k == 0), stop=(k == 7))
            hh, nq = nb // 6, nb % 6
            self.tt("dve", mrows[hh][:, nq * 512:(nq + 1) * 512], pm, adabs[hh][:, nq * 512:(nq + 1) * 512], ALU.add)
        for hh in range(2):
            self.dma("sp", V(self.MODS.ap[:, hh * 3072:(hh + 1) * 3072], self.MODS.b), mrows[hh], join=True)
        self.stage_end()

    def stage_a0(self, l):
        self.stage_begin()
        identf = self.sb([128, 128])
        self.dma("sp", identf, self.ident)
        identb = self.sb([128, 128], BF16)
        self.cp("dve", identb, identf)
        grow = self.load_row_b("sp", self.n1g.ap[l], 1024)
        A = [None, None]
        S = [None, None]
        for seg in (0, 1):
            A[seg] = self.sb([128, 1024])
            self.load_mod_into(A[seg], 1, seg, gtile=grow)
            S[seg] = self.mod_tile(0, seg)
        xts = [self.sb([128, 1024]) for _ in range(2)]
        junk = self.sb([128, 1024])
        t1s = [self.sb([128, 1024]) for _ in range(2)]
        hbs = [self.sb([128, 1024]) for _ in range(2)]
        sss = [self.sb([128, 1]) for _ in range(2)]
        rss = [self.sb([128, 1]) for _ in range(2)]
        hTs = [self.sb([128, 8, 128]) for _ in range(2)]
        pTs = [[self.pbank(2 * k, [128, 4, 128]), self.pbank(2 * k + 1, [128, 4, 128])] for k in range(2)]
        for i in range(self.NT):
            seg = 1 if i < 2 else 0
            p = i % 2
            xt = xts[p]
            self.dma("sp", xt, self.tile_rows(self.XR, i, l, Buf()))
            self.rms_rstd(xt, 1024, sss[p], rss[p], junk=junk)
            self.stt(t1s[p], xt, rss[p], A[seg], ALU.mult, ALU.mult)
            self.tt("pool", hbs[p], t1s[p], S[seg], ALU.add)
            for k in range(8):
                self.tr(pTs[p][k // 4][:, k % 4, :], hbs[p][:, k * 128:(k + 1) * 128], identf)
            self.cp("act", hTs[p][:, 0:4, :], pTs[p][0])
            self.cp("dve", hTs[p][:, 4:8, :], pTs[p][1])
            self.dma("pool", V(self.HT.ap[i], self.HT.b), hTs[p], join=True)
        self.stage_end()

    def stage_a1(self, l):
        self.stage_begin()
        NT = self.NT
        wts = [[self.sb([128, 8, 512]) for _ in range(2)] for _ in range(2)]
        hts = [self.sb([128, 8, 128]) for _ in range(3)]
        ots = [self.sb([128, 1024]) for _ in range(2)]
        wv = self.wT.ap[l].rearrange("(k p) n -> p k n", p=128)
        pbs = [self.pbank(k, [128, 512]) for k in range(4)]
        cnt = 0
        blks = [(self.PTA, 0, c0, min(1024, PTA_W - c0)) for c0 in range(0, PTA_W, 1024)]
        blks += [(self.PTB, PTA_W, c0, min(1024, PTB_W - c0)) for c0 in range(0, PTB_W, 1024)]
        for cb, (PTX, wof, c0, cwid) in enumerate(blks):
            wt = wts[cb % 2]
            for hf in range((cwid + 511) // 512):
                hw_ = min(512, cwid - hf * 512)
                self.dma("pool", wt[hf][:, :, 0:hw_], V(wv[:, :, wof + c0 + hf * 512:wof + c0 + hf * 512 + hw_], self.wT.b))
            for i in range(NT):
                ht = hts[cnt % 3]
                ot = ots[cnt % 2]
                self.dma("sp", ht, V(self.HT.ap[i], self.HT.b))
                nh = (cwid + 511) // 512
                for half in range(nh):
                    n0 = half * 512
                    nw = min(512, cwid - n0)
                    pb = pbs[(cnt * 2 + half) % 4]
                    for k in range(8):
                        self.mm(pb[:, 0:nw], ht[:, k, :], wt[half][:, k, 0:nw], start=(k == 0), stop=(k == 7))
                    if half == 0:
                        self.cp("act", ot[:, n0:n0 + nw], pb[:, 0:nw])
                    else:
                        self.cp("dve", ot[:, n0:n0 + nw], pb[:, 0:nw])
                self.dma("pool", V(PTX.ap[i * 128:(i + 1) * 128, c0:c0 + cwid], PTX.b), ot[:, 0:cwid], join=True)
                cnt += 1
        wfs = [self.sb([128, 8, 512]) for _ in range(2)]
        hgs = [self.sb([128, 8, 512]) for _ in range(2)]
        ofs = [self.sb([128, 4, 512]) for _ in range(2)]
        wfv = self.wF.ap[l].rearrange("(k p) n -> p k n", p=128)
        pfv = self.PF.ap.rearrange("(c p) t -> p c t", p=128)
        groups = [(0, 2)] + [(2 + 4 * g, 4) for g in range((NT - 2) // 4)]
        cnt = 0
        for fb in range(9):
            wf = wfs[fb % 2]
            self.dma("pool", wf, V(wfv[:, :, fb * 512:(fb + 1) * 512], self.wF.b))
            for (ti0, ng) in groups:
                hg = hgs[cnt % 2]
                of = ofs[cnt % 2]
                N = ng * 128
                for j in range(ng):
                    self.dma("sp", hg[:, :, j * 128:(j + 1) * 128], V(self.HT.ap[ti0 + j], self.HT.b), join=(j > 0))
                for c in range(4):
                    pb = pbs[(cnt * 4 + c) % 4]
                    for k in range(8):
                        self.mm(pb[:, 0:N], wf[:, k, c * 128:(c + 1) * 128], hg[:, k, 0:N], start=(k == 0), stop=(k == 7))
                    self.cp("act" if c % 2 == 0 else "dve", of[:, c, 0:N], pb[:, 0:N])
                self.dma("pool", V(pfv[:, fb * 4:(fb + 1) * 4, ti0 * 128:ti0 * 128 + N], self.PF.b), of[:, :, 0:N], join=True)
                cnt += 1
        self.stage_end()

    def stage_conv(self, l):
        self.stage_begin()
        cwt = self.sb([128, 36, 7])
        self.dma("sp", cwt, V(self.cw.ap[l], self.cw.b))
        cbt = self.sb([128, 36])
        self.dma("sp", cbt, V(self.cb.ap[l], self.cb.b))
        identf = self.sb([128, 128])
        self.dma("sp", identf, self.ident)
        WN = 1024
        wins = [(0, 256, 0, 256)]
        for w in range(self.SEQ // WN):
            wins.append((256 + w * WN, WN, 256, self.T))
        xins = [[self.sb([128, WN + 6]) for _ in range(4)] for _ in range(2)]
        accs = [self.sb([128, 4, WN]) for _ in range(2)]
        oqs = [self.sb([128, 512]) for _ in range(2)]
        pqs = [self.pbank(k, [128, 512]) for k in range(2)]
        pfv = self.PF.ap.rearrange("(c p) t -> p c t", p=128)
        cnt = 0
        tcnt = 0
        for g in range(9):
            for (t0, n, s0, s1) in wins:
                xin = xins[cnt % 2]
                acc = accs[cnt % 2]
                cnt += 1
                lo = max(t0 - 3, s0)
                hi = min(t0 + n + 3, s1)
                for c in range(4):
                    if lo > t0 - 3:
                        self.memset("pool", xin[c][:, 0:3], 0.0)
                    if hi < t0 + n + 3:
                        self.memset("pool", xin[c][:, n + 3:n + 6], 0.0)
                    self.dma("sp", xin[c][:, lo - (t0 - 3):hi - (t0 - 3)], V(pfv[:, g * 4 + c, lo:hi], self.PF.b))
                for c in range(4):
                    ch = g * 4 + c
                    self.act(acc[:, c, 0:n], xin[c][:, 0:n], AF.Identity, scale=cwt[:, ch, 0:1])
                for k in range(1, 7):
                    for c in range(4):
                        ch = g * 4 + c
                        self.stt(acc[:, c, 0:n], xin[c][:, k:k + n], cwt[:, ch, k:k + 1], acc[:, c, 0:n], ALU.mult, ALU.add)
                for c in range(4):
                    ch = g * 4 + c
                    self.act(acc[:, c, 0:n], acc[:, c, 0:n], AF.Silu, bias=cbt[:, ch:ch + 1])
                for j in range(n // 128):
                    pq = pqs[tcnt % 2]
                    oq = oqs[tcnt % 2]
                    tcnt += 1
                    for c in range(4):
                        self.tr(pq[:, c * 128:(c + 1) * 128], acc[:, c, j * 128:(j + 1) * 128], identf)
                    self.cp("act" if tcnt % 2 else "dve", oq, pq)
                    r0 = t0 + j * 128
                    self.dma("pool", V(self.QT.ap[r0:r0 + 128, g * 512:(g + 1) * 512], self.QT.b), oq, join=True)
        self.stage_end()

    def stage_merge(self, l):
        self.stage_begin()
        NT = self.NT
        wsrc = [self.w_bm, self.w_br, self.w_bg, self.w_out]
        wb = []
        for j in range(4):
            t = self.sb([128, 8, 1024], BF16)
            self.dma("pool", t, V(wsrc[j].ap[l].rearrange("(k p) n -> p k n", p=128), wsrc[j].b))
            wb.append(t)
        identf = self.sb([128, 128])
        self.dma("sp", identf, self.ident)
        identb = self.sb([128, 128], BF16)
        self.cp("dve", identb, identf)
        mng = self.load_row_b("sp", self.m_norm_g.ap[l], 1024)
        G1 = [self.mod_tile(2, 0), self.mod_tile(2, 1)]
        NB = 2
        ys = [[self.sb([128, 1024]) for _ in range(3)] for _ in range(NB)]
        gts = [self.sb([128, 3072]) for _ in range(NB)]
        xts = [self.sb([128, 1024]) for _ in range(NB)]
        ybs = [[self.sb([128, 1024], BF16) for _ in range(2)] for _ in range(3)]
        yTs = [[self.sb([128, 8, 128], BF16) for _ in range(2)] for _ in range(3)]
        ynf = self.sb([128, 1024])
        mTl = self.sb([128, 8, 128], BF16)
        mixl = self.sb([128, 1024], BF16)
        junk = self.sb([128, 512])
        ss = self.sb([128, 2])
        rs = self.sb([128, 2])
        mixed = self.sb([128, 1024])
        tmp = self.sb([128, 1024])
        mixb = self.sb([128, 1024], BF16)
        mT = self.sb([128, 8, 128], BF16)
        pT = [self.pbank(k, [128, 8, 128], BF16) for k in range(3)]
        pM = [[self.pbank(3, [128, 512]), self.pbank(4, [128, 512])], [self.pbank(5, [128, 512]), self.pbank(6, [128, 512])]]
        pT2 = self.pbank(7, [128, 8, 128], BF16)
        srcs = [self.YM, self.YR, self.YG]
        for i in range(NT):
            seg = 1 if i < 2 else 0
            p = i % NB
            for b in range(3):
                self.dma("sp", ys[p][b], V(srcs[b].ap[i * 128:(i + 1) * 128], srcs[b].b))
            self.dma("sp", gts[p], V(self.PTB.ap[i * 128:(i + 1) * 128, PT_GATE:PTB_W], self.PTB.b))
            xb = Buf()
            self.dma("sp", xts[p], self.tile_rows(self.XR, i, l, xb))
            ym = ys[p][0]
            for g in range(2):
                self.act(junk, ym[:, g * 512:(g + 1) * 512], AF.Square, accum=ss[:, g:g + 1])
            self.act(rs, ss, AF.Sqrt, scale=1.0 / 512, bias=EPS)
            self.recip(rs, rs)
            for g in range(2):
                self.stt(ynf[:, g * 512:(g + 1) * 512], ym[:, g * 512:(g + 1) * 512], rs[:, g:g + 1],
                         mng[:, g * 512:(g + 1) * 512], ALU.mult, ALU.mult)
            fsrc = [ynf, ys[p][1], ys[p][2]]
            for b in range(3):
                self.cp("pool", ybs[b][0], fsrc[b])
                self.tt("pool", ybs[b][1], fsrc[b], ybs[b][0], ALU.subtract)
            for b in range(3):
                for hl in range(2):
                    for k in range(8):
                        self.tr(pT[b][:, k, :], ybs[b][hl][:, k * 128:(k + 1) * 128], identb)
                    self.cp("act" if hl == 0 else "dve", yTs[b][hl], pT[b])
            self.act(gts[p], gts[p], AF.Sigmoid)
            for b in range(3):
                pm = pM[b % 2]
                for half in range(2):
                    hs_ = slice(half * 512, (half + 1) * 512)
                    gs_ = slice(b * 1024 + half * 512, b * 1024 + (half + 1) * 512)
                    for hl in range(2):
                        for k in range(8):
                            self.mm(pm[half], yTs[b][hl][:, k, :], wb[b][:, k, hs_], start=(hl == 0 and k == 0), stop=(hl == 1 and k == 7))
                    if b == 0:
                        self.tt("dve", mixed[:, hs_], pm[half], gts[p][:, gs_], ALU.mult)
                    elif b == 1:
                        self.tt("dve", tmp[:, hs_], pm[half], gts[p][:, gs_], ALU.mult)
                        self.tt("pool", mixed[:, hs_], mixed[:, hs_], tmp[:, hs_], ALU.add)
                    else:
                        self.tt("dve", tmp[:, hs_], pm[half], gts[p][:, gs_], ALU.mult)
                        self.tt("pool", mixed[:, hs_], mixed[:, hs_], tmp[:, hs_], ALU.add)
            self.cp("act", mixb, mixed)
            self.tt("dve", mixl, mixed, mixb, ALU.subtract)
            for k in range(8):
                self.tr(pT2[:, k, :], mixb[:, k * 128:(k + 1) * 128], identb)
            self.cp("act", mT, pT2)
            for k in range(8):
                self.tr(pT2[:, k, :], mixl[:, k * 128:(k + 1) * 128], identb)
            self.cp("act", mTl, pT2)
            pm = pM[1]
            for half in range(2):
                hs_ = slice(half * 512, (half + 1) * 512)
                for hl, mt_ in enumerate((mT, mTl)):
                    for k in range(8):
                        self.mm(pm[half], mt_[:, k, :], wb[3][:, k, hs_], start=(hl == 0 and k == 0), stop=(hl == 1 and k == 7))
                self.tt("dve", tmp[:, hs_], pm[half], G1[seg][:, hs_], ALU.mult)
            self.tt("pool", xts[p], xts[p], tmp, ALU.add)
            self.dma("pool", self.tile_rows(self.XR, i, l, xb), xts[p])
        self.stage_end()

    def stage_ffn(self, l):
        self.stage_begin()
        NT = self.NT
        w1v = self.w_ff1.ap[l].rearrange("(k p) n -> p k n", p=128)
        w1q = []
        for q in range(4):
            t = self.sb([128, 8, 1024], BF16)
            self.dma("pool", t, V(w1v[:, :, q * 1024:(q + 1) * 1024], self.w_ff1.b))
            w1q.append(t)
        w2v = self.w_ff2.ap[l].rearrange("(f p) n -> p f n", p=128)
        w2q = []
        for q in range(4):
            t = self.sb([128, 8, 1024], BF16)
            self.dma("pool", t, V(w2v[:, q * 8:(q + 1) * 8, :], self.w_ff2.b))
            w2q.append(t)
        uT = [self.sb([128, 32, 128], BF16) for _ in range(2)]
        identf = self.sb([128, 128])
        self.dma("sp", identf, self.ident)
        identb = self.sb([128, 128], BF16)
        self.cp("dve", identb, identf)
        grow = self.load_row_b("sp", self.n2g.ap[l], 1024)
        A2 = self.sb([128, 1024])
        S2 = self.sb([128, 1024])
        G2 = self.sb([128, 1024])
        xts = [self.sb([128, 1024]) for _ in range(2)]
        t1 = self.sb([128, 1024])
        hb = [self.sb([128, 1024], BF16) for _ in range(2)]
        hT = [self.sb([128, 8, 128], BF16) for _ in range(2)]
        ss = self.sb([128, 1])
        rs = self.sb([128, 1])
        rl = [self.sb([128, 128]) for _ in range(2)]
        uf = [self.sb([128, 128]) for _ in range(2)]
        pT = [self.pbank(k, [128, 8, 128], BF16) for k in range(2)]
        pU = [self.pbank(2 + k, [128, 128]) for k in range(2)]
        pO = [[self.pbank(4, [128, 512]), self.pbank(5, [128, 512])], [self.pbank(6, [128, 512]), self.pbank(7, [128, 512])]]
        for i in range(NT):
            seg = 1 if i < 2 else 0
            if i == 0 or i == 2:
                self.load_mod_into(A2, 4, seg, gtile=grow)
                self.load_mod_into(S2, 3, seg)
                self.load_mod_into(G2, 5, seg)
            x = xts[i % 2]
            xb = Buf()
            self.dma("sp", x, V(self.XR.ap[i * 128:(i + 1) * 128], xb))
            self.rms_rstd(x, 1024, ss, rs, junk=t1)
            self.stt(t1, x, rs, A2, ALU.mult, ALU.mult)
            self.tt("pool", t1, t1, S2, ALU.add)
            self.cp("act", hb[0], t1)
            self.tt("pool", hb[1], t1, hb[0], ALU.subtract)
            for hl in range(2):
                for k in range(8):
                    self.tr(pT[hl][:, k, :], hb[hl][:, k * 128:(k + 1) * 128], identb)
                self.cp("act" if hl == 0 else "dve", hT[hl], pT[hl])
            for f in range(32):
                pu = pU[f % 2]
                wsl = w1q[f // 8]
                c0 = (f % 8) * 128
                for hl in range(2):
                    for k in range(8):
                        self.mm(pu, wsl[:, k, c0:c0 + 128], hT[hl][:, k, :], start=(hl == 0 and k == 0), stop=(hl == 1 and k == 7))
                r = rl[f % 2]
                u = uf[f % 2]
                self.act(r, pu, AF.Relu)
                self.tt("dve", u, r, r, ALU.mult)
                self.cp("pool", uT[0][:, f, :], u)
                self.tt("pool", uT[1][:, f, :], u, uT[0][:, f, :], ALU.subtract)
            po = pO[i % 2]
            for half in range(2):
                hs_ = slice(half * 512, (half + 1) * 512)
                for hl in range(2):
                    for f in range(32):
                        self.mm(po[half], uT[hl][:, f, :], w2q[f // 8][:, f % 8, hs_], start=(hl == 0 and f == 0), stop=(hl == 1 and f == 31))
                self.tt("dve", t1[:, hs_], po[half], G2[:, hs_], ALU.mult)
            self.tt("pool", x, x, t1, ALU.add)
            self.dma("pool", V(self.XR.ap[i * 128:(i + 1) * 128], xb), x)
        self.stage_end()

    def stage_final(self):
        self.stage_begin()
        fg = self.load_row_b("sp", self.fing.ap[0], 1024)
        xts = [self.sb([128, 1024]) for _ in range(3)]
        junk = self.sb([128, 1024])
        sss = [self.sb([128, 1]) for _ in range(2)]
        rss = [self.sb([128, 1]) for _ in range(2)]
        for j in range(self.SEQ // 128):
            xt = xts[j % 3]
            r0 = self.CTX + j * 128
            self.dma("sp", xt, V(self.XR.ap[r0:r0 + 128], Buf()))
            self.rms_rstd(xt, 1024, sss[j % 2], rss[j % 2], junk=junk)
            self.stt(xt, xt, rss[j % 2], fg, ALU.mult, ALU.mult)
            self.dma("pool", V(self.out.ap[j * 128:(j + 1) * 128], self.out.b), xt, join=True)
        self.stage_end()

    def consts(self):
        cm = self.sb([128, 5, 128])
        self.dma("sp", cm, V(self.cmask.ap.rearrange("m p j -> p m j"), self.cmask.b))
        identf = self.sb([128, 128])
        self.dma("sp", identf, self.ident)
        identb = self.sb([128, 128], BF16)
        self.cp("dve", identb, identf)
        return cm, identf, identb

    def dir_order(self, d):
        NT = self.NT
        return list(range(NT)) if d == 0 else [1, 0] + list(range(NT - 1, 1, -1))

    def stage_mamba(self, l):
        for d in (0, 1):
            self.mamba_pass(l, d)

    def mamba_pass(self, l, d):
        self.stage_begin()
        cm, identf, identb = self.consts()
        U, SL, Lw, SU, ONES = (cm[:, j, :] for j in range(5))
        cs_lhsT, D_lhsT, D_rmask, CBmask = (U, SL, U, U) if d == 0 else (Lw, SU, Lw, Lw)
        prm = self.load_row_b("sp", self.mprm.ap[l], 80)
        dtb = prm[:, d * 16:(d + 1) * 16]
        dsk = prm[:, 64:80]
        aneg = self.sb([128, 16])
        self.act(aneg, prm[:, 32 + d * 16:32 + (d + 1) * 16], AF.Exp)
        self.ts("dve", aneg, aneg, -1.0, ALU.mult)
        H = self.sb([128, 16, 64])
        Hb = self.sb([128, 1024], BF16)
        self.memset("dve", H, 0.0)
        self.memset("dve", Hb, 0.0)
        xs_ = [self.sb([128, 16, 64]) for _ in range(2)]
        bc_ = [self.sb([128, 512]) for _ in range(2)]
        dtr_ = [self.sb([128, 16]) for _ in range(2)]
        y0_ = [self.sb([128, 1024]) for _ in range(2)] if d == 1 else None
        z_ = [self.sb([128, 1024]) for _ in range(2)] if d == 1 else None
        t16 = self.sb([128, 16])
        dt = self.sb([128, 16])
        la = self.sb([128, 16])
        xdt = self.sb([128, 16, 64], BF16)
        bcb = self.sb([128, 512], BF16)
        bcT = self.sb([128, 4, 128], BF16)
        cbm = self.sb([128, 2, 128])
        csb = self.sb([128, 32])
        ecs = self.sb([128, 16])
        etot = self.sb([128, 16])
        dd = self.sb([128, 16])
        edd = self.sb([128, 16])
        dte = self.sb([128, 16])
        rhsall = self.sb([128, 16, 128])
        seg = self.sb([128, 16, 128])
        MT = self.sb([128, 16, 128], BF16)
        yts = [self.sb([128, 16, 64]) for _ in range(2)]
        t2 = self.sb([128, 16, 64])
        xdd = self.sb([128, 16, 64], BF16)
        pTr = self.pbank(0, [128, 4, 128], BF16)
        pCB = self.pbank(1, [128, 2, 128])
        pcs = self.pbank(1, [128, 32], off=256)
        pD = [self.pbank(2 + q, [128, 512]) for q in range(4)]
        py = [self.pbank(6 + g, [128, 8, 64]) for g in range(2)]
        po = [self.pbank(2 + g, [128, 8, 64]) for g in range(2)]
        ps = [self.pbank(4 + g, [128, 8, 64]) for g in range(2)]
        rhs2 = rhsall.rr("p h l -> p (h l)")
        seg2 = seg.rr("p h l -> p (h l)")
        for n, i in enumerate(self.dir_order(d)):
            p = n % 2
            r0 = i * 128
            xs, bc, dtr, yt = xs_[p], bc_[p], dtr_[p], yts[p]
            self.dma("sp", xs.rr("p h q -> p (h q)"), V(self.QT.ap[r0:r0 + 128, QT_X:QT_X + 1024], self.QT.b))
            self.dma("sp", bc, V(self.QT.ap[r0:r0 + 128, QT_B:QT_B + 512], self.QT.b))
            self.dma("sp", dtr, V(self.PTA.ap[r0:r0 + 128, PT_MDT + d * 16:PT_MDT + (d + 1) * 16], self.PTA.b))
            if d == 1:
                self.dma("sp", y0_[p], V(self.Y0.ap[r0:r0 + 128], self.Y0.b))
                self.dma("sp", z_[p], V(self.PTA.ap[r0:r0 + 128, PT_MZ:PT_MZ + 1024], self.PTA.b))
            self.tt("dve", t16, dtr, dtb, ALU.add)
            self.act(t16, t16, AF.Exp)
            self.act(dt, t16, AF.Ln, bias=1.0)
            self.tt("dve", la, dt, aneg, ALU.mult)
            self.tt("pool", xdt, xs, dt.unsq(2).bc([128, 16, 64]), ALU.mult)
            self.cp("act", bcb, bc)
            for j in range(4):
                self.tr(pTr[:, j, :], bcb[:, j * 128:(j + 1) * 128], identb)
            self.cp("act", bcT, pTr)
            for g in range(2):
                self.mm(pCB[:, g, :], bcT[:, g, :], bcT[:, 2 + g, :])
            self.tt("dve", cbm, pCB, CBmask.unsq(1).bc([128, 2, 128]), ALU.mult)
            self.mm(pcs[:, 0:16], cs_lhsT, la)
            self.mm(pcs[:, 16:32], ONES, la)
            self.cp("dve", csb, pcs)
            self.act(ecs, csb[:, 0:16], AF.Exp)
            self.act(etot, csb[:, 16:32], AF.Exp)
            self.tt("dve", dd, csb[:, 16:32], csb[:, 0:16], ALU.subtract)
            self.act(edd, dd, AF.Exp)
            self.tt("dve", rhsall, D_rmask.unsq(1).bc([128, 16, 128]), la.unsq(2).bc([128, 16, 128]), ALU.mult)
            for q in range(4):
                self.mm(pD[q], D_lhsT, rhs2[:, q * 512:(q + 1) * 512])
                self.act(seg2[:, q * 512:(q + 1) * 512], pD[q], AF.Exp)
            for g in range(2):
                self.tt("dve" if g == 0 else "pool", MT[:, g * 8:(g + 1) * 8, :], seg[:, g * 8:(g + 1) * 8, :],
                        cbm[:, g, :].unsq(1).bc([128, 8, 128]), ALU.mult)
            for h in range(16):
                self.mm(py[h // 8][:, h % 8, :], MT[:, h, :], xdt[:, h, :])
            for g in range(2):
                self.mm(po[g].rr("p h q -> p (h q)"), bcT[:, 2 + g, :], Hb[:, g * 512:(g + 1) * 512])
            for g in range(2):
                gs = slice(g * 8, (g + 1) * 8)
                self.tt("dve", yt[:, gs, :], po[g], ecs[:, gs].unsq(2).bc([128, 8, 64]), ALU.mult)
                self.tt("dve", yt[:, gs, :], yt[:, gs, :], py[g], ALU.add)
            self.tt("dve", dte, dt, edd, ALU.mult)
            self.tt("pool", xdd, xs, dte.unsq(2).bc([128, 16, 64]), ALU.mult)
            for g in range(2):
                self.mm(ps[g].rr("p h q -> p (h q)"), bcb[:, g * 128:(g + 1) * 128],
                        xdd[:, g * 8:(g + 1) * 8, :].rr("p h q -> p (h q)"))
            for g in range(2):
                gs = slice(g * 8, (g + 1) * 8)
                self.tt("dve", H[:, gs, :], H[:, gs, :], etot[:, gs].unsq(2).bc([128, 8, 64]), ALU.mult)
                self.tt("dve", H[:, gs, :], H[:, gs, :], ps[g], ALU.add)
            self.cp("act", Hb, H.rr("p h q -> p (h q)"))
            yt2 = yt.rr("p h q -> p (h q)")
            if d == 0:
                self.dma("pool", V(self.Y0.ap[r0:r0 + 128], self.Y0.b), yt2, join=True)
            else:
                self.tt("pool", yt2, yt2, y0_[p], ALU.add)
                self.tt("pool", t2, xs, dsk.unsq(2).bc([128, 16, 64]), ALU.mult)
                self.tt("pool", yt, yt, t2, ALU.add)
                self.act(z_[p], z_[p], AF.Silu)
                self.tt("dve", yt2, yt2, z_[p], ALU.mult)
                self.dma("pool", V(self.YM.ap[r0:r0 + 128], self.YM.b), yt2, join=True)
        self.stage_end()

    def stage_rwkv(self, l):
        self.rwkv_prep(l)
        for d in (0, 1):
            self.rwkv_scan(l, d)

    def rwkv_prep(self, l):
        self.stage_begin()
        NT = self.NT
        cm, identf, identb = self.consts()
        mu = self.load_row_b("sp", self.rmu.ap[l], 3488)
        kkrow = self.load_row_b("sp", self.rrows.ap[l, 0], 1024)
        karow = self.load_row_b("sp", self.rrows.ap[l, 1], 1024)
        rkrow = self.load_row_b("sp", self.rrows.ap[l, 2], 1024)
        w2t = self.sb([65, 2, 1024])
        a2t = self.sb([65, 2, 1024])
        self.dma("sp", w2t, V(self.rw2.ap[l].rearrange("d k n -> k d n"), self.rw2.b))
        self.dma("sp", a2t, V(self.ra2.ap[l].rearrange("d k n -> k d n"), self.ra2.b))
        g2t = self.sb([128, 2, 1024])
        self.dma("sp", g2t[:, 0, :], V(self.rg2.ap[l, 0:128], self.rg2.b))
        self.dma("sp", g2t[0:32, 1, :], V(self.rg2.ap[l, 128:160], self.rg2.b), join=True)
        lh = self.sb([65, 4, 128])
        self.memset("dve", lh[64:65], 1.0)
        U0s = [self.sb([128, 3488]) for _ in range(2)]
        Um = self.sb([128, 3488])
        Up = self.sb([128, 3488])
        tw = self.sb([128, 256])
        lws = [self.sb([128, 1024]) for _ in range(2)]
        as_ = [self.sb([128, 1024]) for _ in range(2)]
        kkr = self.sb([128, 16, 64])
        sq = self.sb([128, 16, 64])
        kk = self.sb([128, 16, 64])
        ssq = self.sb([128, 16])
        rn = self.sb([128, 16])
        tmp = self.sb([128, 1024])
        kds = [self.sb([128, 1024], BF16) for _ in range(2)]
        kas = [self.sb([128, 1024], BF16) for _ in range(2)]
        rb = self.sb([128, 1024], BF16)
        vb = self.sb([128, 1024], BF16)
        kdsum = self.sb([128, 16, 64])
        bs = self.sb([128, 16])
        bonus = self.sb([128, 16, 64])
        sg = self.sb([128, 160])
        sgT = self.sb([128, 2, 128])
        gate = self.sb([128, 1024])
        pl = self.pbank(0, [64, 4, 128])
        pairs = [[self.pbank(1 + 2 * q, [128, 512]), self.pbank(2 + 2 * q, [128, 512])] for q in range(3)]
        psg = self.pbank(7, [128, 2, 128])
        PTR = self.PTA.ap
        for i in range(NT):
            p = i % 2
            r0 = i * 128
            U0 = U0s[p]
            c0, c1 = PT_R, PT_R + 3488
            self.dma("sp", U0, V(PTR[r0:r0 + 128, c0:c1], self.PTA.b))
            if i == 0 or i == 2:
                self.dma("sp", Um[1:128], V(PTR[r0:r0 + 127, c0:c1], self.PTA.b))
                self.dma("sp", Um[0:1], self.zrow, join=True)
            else:
                self.dma("sp", Um, V(PTR[r0 - 1:r0 + 127, c0:c1], self.PTA.b))
            if i == 1 or i == NT - 1:
                self.dma("sp", Up[0:127], V(PTR[r0 + 1:r0 + 128, c0:c1], self.PTA.b))
                self.dma("sp", Up[127:128], self.zrow, join=True)
            else:
                self.dma("sp", Up, V(PTR[r0 + 1:r0 + 129, c0:c1], self.PTA.b))
            self.tt("pool", Um, Um, Up, ALU.add)
            self.stt(Um, Um, 0.5, U0, ALU.mult, ALU.subtract)
            self.tt("pool", Um, Um, mu, ALU.mult)
            self.tt("dve", U0, U0, Um, ALU.add)
            r_, k_, v_ = U0[:, 0:1024], U0[:, 1024:2048], U0[:, 2048:3072]
            self.act(tw[:, 0:128], U0[:, 3072:3200], AF.Tanh)
            self.cp("pool", tw[:, 128:256], U0[:, 3200:3328])
            for j in range(4):
                self.tr(pl[:, j, :], tw[:, j * 64:(j + 1) * 64], identf)
            self.cp("act", lh[0:64], pl)
            for d in range(2):
                pw = pairs[d]
                for half in range(2):
                    self.mm(pw[half], lh[:, d, :], w2t[:, d, half * 512:(half + 1) * 512])
                    self.act(lws[d][:, half * 512:(half + 1) * 512], pw[half], AF.Sigmoid)
                self.ts("pool", lws[d], lws[d], -float(np.exp(-0.5)), ALU.mult)
                pa = pairs[2] if d == 0 else pairs[0]
                for half in range(2):
                    self.mm(pa[half], lh[:, 2 + d, :], a2t[:, d, half * 512:(half + 1) * 512])
                    self.act(as_[d][:, half * 512:(half + 1) * 512], pa[half], AF.Sigmoid)
            kkr2 = kkr.rr("p h q -> p (h q)")
            kk2 = kk.rr("p h q -> p (h q)")
            self.tt("pool", kkr2, k_, kkrow, ALU.mult)
            self.tt("pool", sq, kkr, kkr, ALU.mult)
            self.red(ssq, sq)
            self.act(rn, ssq, AF.Sqrt, bias=EPS)
            self.recip(rn, rn)
            self.tt("dve", kk, kkr, rn.unsq(2).bc([128, 16, 64]), ALU.mult)
            for d in range(2):
                self.stt(tmp, as_[d], -1.0, karow, ALU.add, ALU.mult)
                self.stt(kds[d], tmp, 1.0, k_, ALU.add, ALU.mult)
                self.tt("pool", kas[d], kk2, as_[d], ALU.mult)
            kdsum2 = kdsum.rr("p h q -> p (h q)")
            self.tt("pool", kdsum2, kds[0], kds[1], ALU.add)
            self.tt("pool", kdsum2, kdsum2, r_, ALU.mult)
            self.tt("pool", kdsum2, kdsum2, rkrow, ALU.mult)
            self.red(bs, kdsum)
            self.tt("dve", bonus, v_.rr("p (h q) -> p h q", h=16), bs.unsq(2).bc([128, 16, 64]), ALU.mult)
            self.cp("act", rb, r_)
            self.cp("act", vb, v_)
            self.act(sg, U0[:, 3328:3488], AF.Sigmoid)
            self.tr(psg[:, 0, :], sg[:, 0:128], identf)
            self.tr(psg[0:32, 1, :], sg[:, 128:160], identf)
            self.cp("dve", sgT[:, 0, :], psg[:, 0, :])
            self.cp("dve", sgT[0:32, 1, :], psg[0:32, 1, :])
            pg = pairs[1]
            for half in range(2):
                self.mm(pg[half], sgT[:, 0, :], g2t[:, 0, half * 512:(half + 1) * 512], start=True, stop=False)
                self.mm(pg[half], sgT[0:32, 1, :], g2t[0:32, 1, half * 512:(half + 1) * 512], start=False, stop=True)
                self.cp("act", gate[:, half * 512:(half + 1) * 512], pg[half])
            rows = slice(r0, r0 + 128)
            for (dst, src) in ((self.RSr, rb), (self.RSv, vb), (self.RSkk, kk2), (self.RSgate, gate),
                               (self.RSbonus, bonus.rr("p h q -> p (h q)")),
                               (self.RSlw[0], lws[0]), (self.RSlw[1], lws[1]), (self.RSkd[0], kds[0]), (self.RSkd[1], kds[1]),
                               (self.RSka[0], kas[0]), (self.RSka[1], kas[1])):
                self.dma("pool", V(dst.ap[rows], dst.b), src, join=True)
        self.stage_end()

    def rwkv_scan(self, l, d):
        self.stage_begin()
        cm, identf, identb = self.consts()
        U, SL, Lw, SU, ONES = (cm[:, j, :] for j in range(5))
        cI, cA, cB, Ms, Mi = (U, SL, SU, SU, U) if d == 0 else (Lw, SU, SL, SL, Lw)
        lmt = self.load_lmt(d)
        if d == 1:
            lng = self.load_row_b("sp", self.rrows.ap[l, 3], 1024)
            lnb = self.load_row_b("sp", self.rrows.ap[l, 4], 1024)
        Hs = self.sb([64, 16, 64])
        Hb = self.sb([64, 16, 64], BF16)
        self.memset("dve", Hs, 0.0)
        self.memset("dve", Hb, 0.0)
        lw_ = [self.sb([128, 1024]) for _ in range(2)]
        kk_ = [self.sb([128, 16, 64]) for _ in range(2)]
        r_ = [self.sb([128, 1024], BF16) for _ in range(2)]
        v_ = [self.sb([128, 16, 64], BF16) for _ in range(2)]
        kd_ = [self.sb([128, 1024], BF16) for _ in range(2)]
        ka_ = [self.sb([128, 1024], BF16) for _ in range(2)]
        if d == 1:
            y0 = self.sb([128, 16, 64])
            gt = self.sb([128, 1024])
            bn = self.sb([128, 1024])
        Et = [self.sb([128, 1024]) for _ in range(2)]
        Rt = self.sb([128, 1024], BF16)
        Kt = self.sb([128, 1024], BF16)
        Bt = self.sb([128, 1024], BF16)
        At = self.sb([128, 1024], BF16)
        Kh = self.sb([128, 16, 64], BF16)
        Bh = self.sb([128, 16, 64], BF16)
        fm = {nm: self.sb([64, 16, 128], BF16) for nm in ("A", "B", "K", "R")}
        AT = [self.sb([128, 16, 128]) for _ in range(2)]
        A = [self.sb([128, 16, 128]) for _ in range(2)]
        AakT = self.sb([128, 16, 128], BF16)
        ArbT = self.sb([128, 16, 128], BF16)
        ArkT = self.sb([128, 16, 128], BF16)
        Rr = self.sb([128, 16, 128])
        X = [self.sb([128, 16, 128]) for _ in range(2)]
        Wt = self.sb([128, 16, 64], BF16)
        WtT = self.sb([64, 16, 128], BF16)
        Ub = self.sb([128, 16, 64], BF16)
        GLT = self.sb([64, 16])
        yts = [self.sb([128, 16, 64]) for _ in range(2)]
        if d == 1:
            s16 = self.sb([128, 16])
            m16 = self.sb([128, 16])
            yc = Et[0].rr("p (h q) -> p h q", h=16)
            sq = Et[1].rr("p (h q) -> p h q", h=16)
        for n, i in enumerate(self.dir_order(d)):
            p = n % 2
            rows = slice(i * 128, (i + 1) * 128)
            lw, kk, rb, vb, kd, ka, yt = lw_[p], kk_[p], r_[p], v_[p], kd_[p], ka_[p], yts[p]
            self.dma("sp", lw, V(self.RSlw[d].ap[rows], self.RSlw[d].b))
            self.dma("sp", kk.rr("p h q -> p (h q)"), V(self.RSkk.ap[rows], self.RSkk.b))
            self.dma("sp", rb, V(self.RSr.ap[rows], self.RSr.b))
            self.dma("sp", vb.rr("p h q -> p (h q)"), V(self.RSv.ap[rows], self.RSv.b))
            self.dma("sp", kd, V(self.RSkd[d].ap[rows], self.RSkd[d].b))
            self.dma("sp", ka, V(self.RSka[d].ap[rows], self.RSka[d].b))
            if d == 1:
                self.dma("sp", y0.rr("p h q -> p (h q)"), V(self.Y0.ap[rows], self.Y0.b))
                self.dma("sp", gt, V(self.RSgate.ap[rows], self.RSgate.b))
                self.dma("sp", bn, V(self.RSbonus.ap[rows], self.RSbonus.b))
            pI = [self.pbank(0, [128, 512]), self.pbank(1, [128, 512])]
            pA = [self.pbank(2, [128, 512]), self.pbank(3, [128, 512])]
            pB = [self.pbank(4, [128, 512]), self.pbank(5, [128, 512])]
            for half in range(2):
                cs = slice(half * 512, (half + 1) * 512)
                self.mm(pI[half], cI, lw[:, cs])
                self.mm(pA[half], cA, lw[:, cs])
                self.mm(pB[half], cB, lw[:, cs])
            pG = self.pbank(6, [64, 16])
            for j in range(16):
                self.mm(pG[:, j:j + 1], lw[:, j * 64:(j + 1) * 64], ONES[:, 0:1])
            self.act(GLT, pG, AF.Exp)
            E1, E2 = Et
            for half in range(2):
                cs = slice(half * 512, (half + 1) * 512)
                self.act(E1[:, cs], pI[half], AF.Exp)
                self.act(E2[:, cs], pI[half], AF.Exp, scale=-1.0)
            self.tt("pool", Rt, rb, E1, ALU.mult)
            self.tt("dve", Kt, kd, E2, ALU.mult)
            self.tt("pool", Bt, ka, E2, ALU.mult)
            E3, E4 = Et
            for half in range(2):
                cs = slice(half * 512, (half + 1) * 512)
                self.act(E3[:, cs], pA[half], AF.Exp)
                self.act(E4[:, cs], pB[half], AF.Exp)
            self.tt("pool", Kh.rr("p h q -> p (h q)"), kd, E3, ALU.mult)
            self.tt("dve", Bh.rr("p h q -> p (h q)"), ka, E3, ALU.mult)
            self.stt(Rr[:, :, 64:128], kk, -1.0, E4.rr("p (h q) -> p h q", h=16), ALU.mult, ALU.mult)
            self.cp("pool", At.rr("p (h q) -> p h q", h=16), Rr[:, :, 64:128])
            for bi, (nm, src) in enumerate((("A", At), ("B", Bt), ("K", Kt), ("R", Rt))):
                for q in range(2):
                    pt = self.pbank((bi * 2 + q) % 8, [64, 8, 128], BF16)
                    for j in range(8):
                        h = q * 8 + j
                        self.tr(pt[:, j, :], src[:, h * 64:(h + 1) * 64], identb)
                    self.cp("act" if q == 0 else "dve", fm[nm][:, q * 8:(q + 1) * 8, :], pt)

            def fmh(nm, h):
                return fm[nm][:, h, :]

            stop = getattr(self, "rs_stop", 99)
            if stop <= 1:
                continue

            for g in range(4):
                hs = slice(g * 4, g * 4 + 4)
                pab = self.pbank(4, [128, 4, 128])
                pak = self.pbank(5, [128, 4, 128])
                prb = self.pbank(6, [128, 4, 128])
                prk = self.pbank(7, [128, 4, 128])
                for j in range(4):
                    h = g * 4 + j
                    self.mm(pab[:, j, :], fmh("B", h), fmh("A", h))
                    self.mm(pak[:, j, :], fmh("K", h), fmh("A", h))
                    self.mm(prb[:, j, :], fmh("B", h), fmh("R", h))
                    self.mm(prk[:, j, :], fmh("K", h), fmh("R", h))
                msk_s = Ms.unsq(1).bc([128, 4, 128])
                msk_i = Mi.unsq(1).bc([128, 4, 128])
                self.tt("dve", AT[0][:, hs, :], pab, msk_s, ALU.mult)
                self.tt("dve", AakT[:, hs, :], pak, msk_s, ALU.mult)
                self.tt("dve", ArbT[:, hs, :], prb, msk_i, ALU.mult)
                self.tt("dve", ArkT[:, hs, :], prk, msk_i, ALU.mult)
            if stop <= 2:
                continue
            for q in range(2):
                pR = self.pbank(q, [128, 8, 64])
                for j in range(8):
                    h = q * 8 + j
                    self.mm(pR[:, j, :], AakT[:, h, :], vb[:, h, :])
                self.cp("act", Rr[:, q * 8:(q + 1) * 8, 0:64], pR)
            for g in range(4):
                pa = self.pbank(2 + g % 2, [128, 4, 128])
                for j in range(4):
                    h = g * 4 + j
                    self.tr(pa[:, j, :], AT[0][:, h, :], identf)
                self.cp("act", A[0][:, g * 4:g * 4 + 4, :], pa)
            if stop <= 3:
                continue
            Xf = self.chain2(AT[0], Rr, X[0], 16, 128, d, AT[1], A[0], A[1], X[1], identf, lmt)
            if stop <= 4:
                continue
            self.cp("pool", Wt, Xf[:, :, 64:128])
            for q in range(2):
                pt = self.pbank(q, [64, 8, 128], BF16)
                for j in range(8):
                    self.tr(pt[:, j, :], Wt[:, q * 8 + j, :], identb)
                self.cp("act", WtT[:, q * 8:(q + 1) * 8, :], pt)
            for q in range(2):
                pU = self.pbank(1 + q, [128, 8, 64])
                for j in range(8):
                    h = q * 8 + j
                    self.mm(pU[:, j, :], WtT[:, h, :], Hb[:, h, :])
                self.tt("dve", Ub[:, q * 8:(q + 1) * 8, :], Xf[:, q * 8:(q + 1) * 8, 0:64], pU, ALU.add)
            if stop <= 5:
                continue
            for q in range(2):
                pY = self.pbank(3 + q, [128, 8, 64])
                for j in range(8):
                    h = q * 8 + j
                    self.mm(pY[:, j, :], fm["R"][:, h, :], Hb[:, h, :], start=True, stop=False)
                    self.mm(pY[:, j, :], ArbT[:, h, :], Ub[:, h, :], start=False, stop=False)
                    self.mm(pY[:, j, :], ArkT[:, h, :], vb[:, h, :], start=False, stop=True)
                self.cp("act", yt[:, q * 8:(q + 1) * 8, :], pY)
            if stop <= 6:
                continue
            self.tt("dve", Hs, Hs, GLT.unsq(2).bc([64, 16, 64]), ALU.mult)
            for q in range(2):
                pH = self.pbank(5 + q, [64, 8, 64])
                for j in range(8):
                    h = q * 8 + j
                    self.mm(pH[:, j, :], Bh[:, h, :], Ub[:, h, :], start=True, stop=False)
                    self.mm(pH[:, j, :], Kh[:, h, :], vb[:, h, :], start=False, stop=True)
                self.tt("dve", Hs[:, q * 8:(q + 1) * 8, :], Hs[:, q * 8:(q + 1) * 8, :], pH, ALU.add)
            self.cp("act", Hb, Hs)
            yt2 = yt.rr("p h q -> p (h q)")
            if d == 0:
                self.dma("pool", V(self.Y0.ap[rows], self.Y0.b), yt2, join=True)
            else:
                self.tt("pool", yt, yt, y0, ALU.add)
                self.red(s16, yt)
                self.ts("dve", m16, s16, 1.0 / 64, ALU.mult)
                self.tt("dve", yc, yt, m16.unsq(2).bc([128, 16, 64]), ALU.subtract)
                self.tt("pool", sq, yc, yc, ALU.mult)
                self.red(s16, sq)
                self.act(m16, s16, AF.Sqrt, scale=1.0 / 64, bias=64e-5)
                self.recip(m16, m16)
                self.tt("dve", yc, yc, m16.unsq(2).bc([128, 16, 64]), ALU.mult)
                yc2 = yc.rr("p h q -> p (h q)")
                self.tt("pool", yc2, yc2, lng, ALU.mult)
                self.tt("pool", yc2, yc2, lnb, ALU.add)
                self.tt("pool", yc2, yc2, bn, ALU.add)
                self.tt("dve", yc2, yc2, gt, ALU.mult)
                self.dma("pool", V(self.YR.ap[rows], self.YR.b), yc2, join=True)
        self.stage_end()

    CH_DT = None

    def chd(self, v):
        if self.CH_DT is None:
            return v
        return V(v.ap.bitcast(self.CH_DT), v.b)

    def load_lmt(self, d):
        lmt = self.sb([128, 7, 128])
        self.dma("sp", lmt, V(self.lmask.ap[d].rearrange("v p j -> p v j"), self.lmask.b))
        return lmt

    def chain2(self, AT0, R, X, H, W, d, T, Wm, P, AO, identf, lmt):
        ngr = H // 4
        idb = identf.unsq(1).bc([128, 4, 128])
        for g0 in range(0, ngr, 2):
            grs = [g for g in (g0, g0 + 1) if g < ngr]
            pv = {}
            for g in grs:
                base = (g % 2) * 4
                pv[g] = [self.pbank(base + q, [128, 4, 128]) for q in range(4)]
            for g in grs:
                pP = pv[g][0]
                hs = slice(g * 4, g * 4 + 4)
                self.tt("pool", AO[:, hs, :], AT0[:, hs, :], lmt[:, 0, :].unsq(1).bc([128, 4, 128]), ALU.mult)
                for j in range(4):
                    h = g * 4 + j
                    self.mm(pP[:, j, :], AO[:, h, :], identf)
                self.tt("dve", T[:, hs, :], pP, idb, ALU.add)
                self.tt("pool", Wm[:, hs, :], AO[:, hs, :], idb, ALU.add)
            for lev in range(1, 7):
                last = lev == 6
                for g in grs:
                    pP = pv[g][0]
                    hs = slice(g * 4, g * 4 + 4)
                    self.tt("pool", AO[:, hs, :], AT0[:, hs, :], lmt[:, lev, :].unsq(1).bc([128, 4, 128]), ALU.mult)
                    for j in range(4):
                        h = g * 4 + j
                        self.mm(pP[:, j, :], AO[:, h, :], T[:, h, :])
                    self.cp("act", P[:, hs, :], pP)
                for g in grs:
                    pT, pW = pv[g][1], pv[g][2]
                    hs = slice(g * 4, g * 4 + 4)
                    if not last:
                        for j in range(4):
                            h = g * 4 + j
                            self.mm(pT[:, j, :], Wm[:, h, :], P[:, h, :])
                        self.tt("dve", T[:, hs, :], T[:, hs, :], pT, ALU.add)
                    for j in range(4):
                        h = g * 4 + j
                        self.mm(pW[:, j, :], P[:, h, :], Wm[:, h, :])
                    self.tt("dve", Wm[:, hs, :], Wm[:, hs, :], pW, ALU.add)
            for g in grs:
                base = (g % 2) * 4
                if W == 256:
                    pxs = [self.pbank(base + 1, [128, 2, W]), self.pbank(base + 3, [128, 2, W])]
                else:
                    pxs = [self.pbank(base + 3, [128, 4, W])]
                npx = 4 // len(pxs)
                for j in range(4):
                    h = g * 4 + j
                    self.mm(pxs[j // npx][:, j % npx, :], Wm[:, h, :], R[:, h, :])
                for xi, px in enumerate(pxs):
                    hx = slice(g * 4 + xi * npx, g * 4 + (xi + 1) * npx)
                    self.cp("act", X[:, hx, :], px)
        return X

    def chain(self, AT, A, R, X, H, W):
        c = self.chd
        ngr = H // 4
        for g0 in range(0, ngr, 2):
            grs = [g for g in (g0, g0 + 1) if g < ngr]
            views = {}
            for g in grs:
                base = (g % 2) * 4
                if W == 256:
                    pxs = [self.pbank(base + 2, [128, 2, W]), self.pbank(base + 3, [128, 2, W])]
                else:
                    pxs = [self.pbank(base + 2, [128, 4, W])]
                views[g] = (self.pbank(base, [128, 4, 128]), self.pbank(base + 1, [128, 4, 128]), pxs)
            for g in grs:
                pM, pMT, pX = views[g]
                hs = slice(g * 4, g * 4 + 4)
                npx = 4 // len(pX)
                for j in range(4):
                    h = g * 4 + j
                    self.mm(pX[j // npx][:, j % npx, :], c(AT[0][:, h, :]), c(R[:, h, :]))
                for xi, px in enumerate(pX):
                    hx = slice(g * 4 + xi * npx, g * 4 + (xi + 1) * npx)
                    self.tt("dve", X[0][:, hx, :], R[:, hx, :], px, ALU.add)
            cur = 0
            for k in range(6):
                nxt = 1 - cur
                for g in grs:
                    pM, pMT, pX = views[g]
                    hs = slice(g * 4, g * 4 + 4)
                    for j in range(4):
                        h = g * 4 + j
                        self.mm(pMT[:, j, :], c(A[cur][:, h, :]), c(AT[cur][:, h, :]))
                    self.cp("act", AT[nxt][:, hs, :], pMT)
                    if k < 5:
                        for j in range(4):
                            h = g * 4 + j
                            self.mm(pM[:, j, :], c(AT[cur][:, h, :]), c(A[cur][:, h, :]))
                        self.cp("act", A[nxt][:, hs, :], pM)
                for g in grs:
                    pM, pMT, pX = views[g]
                    hs = slice(g * 4, g * 4 + 4)
                    npx = 4 // len(pX)
                    for j in range(4):
                        h = g * 4 + j
                        self.mm(pX[j // npx][:, j % npx, :], c(AT[nxt][:, h, :]), c(X[cur][:, h, :]))
                    for xi, px in enumerate(pX):
                        hx = slice(g * 4 + xi * npx, g * 4 + (xi + 1) * npx)
                        self.tt("dve", X[nxt][:, hx, :], X[cur][:, hx, :], px, ALU.add)
                cur = nxt
        return X[0]

    def stage_gdn(self, l):
        for d in (0, 1):
            self.gdn_pass(l, d)

    def gdn_pass(self, l, d):
        self.stage_begin()
        cm, identf, identb = self.consts()
        U, SL, Lw, SU, ONES = (cm[:, j, :] for j in range(5))
        cs_lhsT, D_lhsT, D_rmask, Ms, Mi = (U, SL, U, SU, U) if d == 0 else (Lw, SU, Lw, SL, Lw)
        lmt = self.load_lmt(d)
        prm = self.load_row_b("sp", self.gprm.ap[l], 160)
        dtb = prm[:, d * 8:(d + 1) * 8]
        ng = prm[:, 32:160]
        aneg = self.sb([128, 8])
        self.act(aneg, prm[:, 16 + d * 8:16 + (d + 1) * 8], AF.Exp)
        self.ts("dve", aneg, aneg, -1.0, ALU.mult)
        S = self.sb([128, 8, 128])
        Sb = self.sb([128, 8, 128], BF16)
        self.memset("dve", S, 0.0)
        self.memset("dve", Sb, 0.0)
        qkv_ = [self.sb([128, 3, 8, 128]) for _ in range(2)]
        ba_ = [self.sb([128, 2, 8]) for _ in range(2)]
        y0_ = [self.sb([128, 8, 128]) for _ in range(2)] if d == 1 else None
        z_ = [self.sb([128, 8, 128]) for _ in range(2)] if d == 1 else None
        sq = self.sb([128, 2, 8, 128])
        ssq = self.sb([128, 16])
        rn = self.sb([128, 16])
        t8 = self.sb([128, 8])
        beta = self.sb([128, 8])
        gg = self.sb([128, 8])
        csb = self.sb([128, 16])
        egcs = self.sb([128, 8])
        edl = self.sb([128, 8])
        dl = self.sb([128, 8])
        bke = self.sb([128, 8])
        kf = self.sb([128, 8, 128])
        qh = self.sb([128, 8, 128], BF16)
        kh = self.sb([128, 8, 128], BF16)
        kb = self.sb([128, 8, 128], BF16)
        ke = self.sb([128, 8, 128], BF16)
        qT = self.sb([128, 8, 128], BF16)
        kT = self.sb([128, 8, 128], BF16)
        kbT = self.sb([128, 8, 128], BF16)
        rhsall = self.sb([128, 8, 128])
        seg = self.sb([128, 8, 128])
        segI = self.sb([128, 8, 128])
        AT = [self.sb([128, 8, 128]) for _ in range(2)]
        A = [self.sb([128, 8, 128]) for _ in range(2)]
        R = self.sb([128, 8, 256])
        X = [self.sb([128, 8, 256]) for _ in range(2)]
        aqkT = self.sb([128, 8, 128], BF16)
        wkT = self.sb([128, 8, 128], BF16)
        vn = self.sb([128, 8, 128], BF16)
        ots = [self.sb([128, 8, 128]) for _ in range(2)]
        rhs2 = rhsall.rr("p h l -> p (h l)")
        seg2 = seg.rr("p h l -> p (h l)")
        for n, i in enumerate(self.dir_order(d)):
            p = n % 2
            r0 = i * 128
            qkv, ba, ot = qkv_[p], ba_[p], ots[p]
            self.dma("sp", qkv.rr("p a h q -> p (a h q)"), V(self.QT.ap[r0:r0 + 128, QT_GQ:QT_W], self.QT.b))
            self.dma("sp", ba[:, 0, :], V(self.PTB.ap[r0:r0 + 128, PT_GBA + d * 8:PT_GBA + (d + 1) * 8], self.PTB.b))
            self.dma("sp", ba[:, 1, :], V(self.PTB.ap[r0:r0 + 128, PT_GBA + 16 + d * 8:PT_GBA + 16 + (d + 1) * 8], self.PTB.b), join=True)
            if d == 1:
                self.dma("sp", y0_[p].rr("p h q -> p (h q)"), V(self.Y0.ap[r0:r0 + 128], self.Y0.b))
                self.dma("sp", z_[p].rr("p h q -> p (h q)"), V(self.PTB.ap[r0:r0 + 128, PT_GZ:PT_GZ + 1024], self.PTB.b))
            qf, kff, vf = qkv[:, 0], qkv[:, 1], qkv[:, 2]
            self.tt("pool", sq, qkv[:, 0:2], qkv[:, 0:2], ALU.mult)
            self.red(ssq, sq.rr("p a h q -> p (a h) q"))
            self.act(rn, ssq, AF.Sqrt, bias=EPS)
            self.recip(rn, rn)
            self.ts("dve", rn[:, 0:8], rn[:, 0:8], 128.0 ** -0.5, ALU.mult)
            self.tt("dve", qh, qf, rn[:, 0:8].unsq(2).bc([128, 8, 128]), ALU.mult)
            self.tt("dve", kf, kff, rn[:, 8:16].unsq(2).bc([128, 8, 128]), ALU.mult)
            self.cp("act", kh, kf)
            self.act(beta, ba[:, 0, :], AF.Sigmoid)
            self.tt("dve", t8, ba[:, 1, :], dtb, ALU.add)
            self.act(t8, t8, AF.Exp)
            self.act(t8, t8, AF.Ln, bias=1.0)
            self.tt("dve", gg, t8, aneg, ALU.mult)
            pcs = self.pbank(7, [128, 16], off=480)
            self.mm(pcs[:, 0:8], cs_lhsT, gg)
            self.mm(pcs[:, 8:16], ONES, gg)
            self.cp("dve", csb, pcs)
            self.act(egcs, csb[:, 0:8], AF.Exp)
            self.act(dl, csb[:, 8:16], AF.Exp)
            self.tt("dve", edl, csb[:, 8:16], csb[:, 0:8], ALU.subtract)
            self.act(edl, edl, AF.Exp)
            self.tt("dve", bke, beta, egcs, ALU.mult)
            self.tt("pool", kb, kf, beta.unsq(2).bc([128, 8, 128]), ALU.mult)
            self.tt("pool", ke, kf, edl.unsq(2).bc([128, 8, 128]), ALU.mult)
            self.tt("pool", R[:, :, 0:128], vf, beta.unsq(2).bc([128, 8, 128]), ALU.mult)
            self.tt("pool", R[:, :, 128:256], kf, bke.unsq(2).bc([128, 8, 128]), ALU.mult)
            for (src, dst, bank) in ((qh, qT, 0), (kh, kT, 1), (kb, kbT, 2)):
                pt = self.pbank(bank, [128, 8, 128], BF16)
                for h in range(8):
                    self.tr(pt[:, h, :], src[:, h, :], identb)
                self.cp("act", dst, pt)
            self.tt("dve", rhsall, D_rmask.unsq(1).bc([128, 8, 128]), gg.unsq(2).bc([128, 8, 128]), ALU.mult)
            for q in range(2):
                pD = self.pbank(3 + q, [128, 512])
                self.mm(pD, D_lhsT, rhs2[:, q * 512:(q + 1) * 512])
                self.act(seg2[:, q * 512:(q + 1) * 512], pD, AF.Exp)
            self.tt("pool", segI, seg, Mi.unsq(1).bc([128, 8, 128]), ALU.mult)
            self.stt(seg, seg, -1.0, Ms.unsq(1).bc([128, 8, 128]), ALU.mult, ALU.mult)
            for q in range(2):
                pk = self.pbank(5, [128, 4, 128])
                pq = self.pbank(6, [128, 4, 128])
                for j in range(4):
                    h = q * 4 + j
                    self.mm(pk[:, j, :], kT[:, h, :], kbT[:, h, :])
                    self.mm(pq[:, j, :], kT[:, h, :], qT[:, h, :])
                hs = slice(q * 4, q * 4 + 4)
                self.tt("dve", AT[0][:, hs, :], pk, seg[:, hs, :], ALU.mult)
                self.tt("dve", aqkT[:, hs, :], pq, segI[:, hs, :], ALU.mult)
            for q in range(2):
                pa = self.pbank(3 + q, [128, 4, 128])
                for j in range(4):
                    h = q * 4 + j
                    self.tr(pa[:, j, :], AT[0][:, h, :], identf)
                self.cp("act", A[0][:, q * 4:q * 4 + 4, :], pa)
            Xf = self.chain2(AT[0], R, X[0], 8, 256, d, AT[1], A[0], A[1], X[1][:, :, 0:128], identf, lmt)
            for q in range(2):
                pw = self.pbank(q * 4, [128, 4, 128])
                for j in range(4):
                    h = q * 4 + j
                    self.tr(pw[:, j, :], Xf[:, h, 128:256], identf)
                self.cp("act", wkT[:, q * 4:q * 4 + 4, :], pw)
            for q in range(2):
                hs = slice(q * 4, q * 4 + 4)
                base = q * 4
                pv = self.pbank(base, [128, 4, 128])
                p1 = self.pbank(base + 1, [128, 4, 128])
                p2 = self.pbank(base + 2, [128, 4, 128])
                pS = self.pbank(base + 3, [128, 4, 128])
                for j in range(4):
                    h = q * 4 + j
                    self.mm(pv[:, j, :], wkT[:, h, :], Sb[:, h, :])
                    self.mm(p1[:, j, :], qT[:, h, :], Sb[:, h, :])
                self.tt("dve", vn[:, hs, :], Xf[:, hs, 0:128], pv, ALU.subtract)
                for j in range(4):
                    h = q * 4 + j
                    self.mm(p2[:, j, :], aqkT[:, h, :], vn[:, h, :])
                    self.mm(pS[:, j, :], ke[:, h, :], vn[:, h, :])
                self.tt("dve", ot[:, hs, :], p1, egcs[:, hs].unsq(2).bc([128, 4, 128]), ALU.mult)
                self.tt("dve", ot[:, hs, :], ot[:, hs, :], p2, ALU.add)
                self.tt("dve", S[:, hs, :], S[:, hs, :], dl[:, hs].unsq(2).bc([128, 4, 128]), ALU.mult)
                self.tt("dve", S[:, hs, :], S[:, hs, :], pS, ALU.add)
                self.cp("act", Sb[:, hs, :], S[:, hs, :])
            ot2 = ot.rr("p h q -> p (h q)")
            if d == 0:
                self.dma("pool", V(self.Y0.ap[r0:r0 + 128], self.Y0.b), ot2, join=True)
            else:
                self.tt("pool", ot, ot, y0_[p], ALU.add)
                self.tt("pool", sq[:, 0], ot, ot, ALU.mult)
                self.red(ssq[:, 0:8], sq[:, 0])
                self.act(rn[:, 0:8], ssq[:, 0:8], AF.Sqrt, scale=1.0 / 128, bias=EPS)
                self.recip(rn[:, 0:8], rn[:, 0:8])
                self.tt("dve", ot, ot, rn[:, 0:8].unsq(2).bc([128, 8, 128]), ALU.mult)
                self.tt("pool", ot, ot, ng.unsq(1).bc([128, 8, 128]), ALU.mult)
                self.act(z_[p], z_[p], AF.Silu)
                self.tt("dve", ot, ot, z_[p], ALU.mult)
                self.dma("pool", V(self.YG.ap[r0:r0 + 128], self.YG.b), ot2, join=True)
        self.stage_end()

    def build(self, stages=None):
        L = self.L
        if stages is None:
            stages = ["init"]
            for l in range(L):
                stages += [("ada", l), ("a0", l), ("a1", l), ("conv", l), ("mamba", l), ("rwkv", l), ("gdn", l),
                           ("merge", l), ("ffn", l)]
            stages.append("final")
        for sg in stages:
            if sg == "init":
                self.stage_init()
            elif sg == "final":
                self.stage_final()
            else:
                getattr(self, "stage_" + sg[0])(sg[1])
        return self.finish()


TM_COLS = np.r_[2592:6080, 0:1024, 2560:2592, 9152:10176, 10176:10208, 10208:13280]
FM_COLS = np.r_[1024:2560, 6080:9152]


def prep_shared(inp, L, l0=0):
    f = lambda a: np.ascontiguousarray(np.asarray(a, dtype=np.float32))
    inp = {k: (np.asarray(v)[l0:l0 + L] if (k not in ("x", "c", "ctx", "c_ctx", "final_g")) else v) for k, v in inp.items()}
    sh = {}
    for k in ("ada_w", "ada_b", "norm1_g", "norm2_g", "w_bm", "w_br", "w_bg", "w_out", "w_ff1", "w_ff2", "m_norm_g"):
        sh[k] = f(inp[k])[:L]
    sh["final_g"] = f(inp["final_g"]).reshape(1, 1024)
    w_in = f(inp["w_in"])[:L]
    sh["wT"] = np.ascontiguousarray(w_in[:, :, TM_COLS])
    sh["wF"] = np.ascontiguousarray(w_in[:, :, FM_COLS])
    cwf = np.concatenate([f(inp["m_conv_w"])[:L], f(inp["g_conv_w"])[:L]], axis=2)
    sh["cw"] = np.ascontiguousarray(cwf.reshape(L, 7, 36, 128).transpose(0, 3, 2, 1))
    cbf = np.concatenate([f(inp["m_conv_b"])[:L], np.zeros((L, 3072), np.float32)], axis=1)
    sh["cb"] = np.ascontiguousarray(cbf.reshape(L, 36, 128).transpose(0, 2, 1))
    sh["ident"] = np.eye(128, dtype=np.float32)
    j = np.arange(128)
    U = (j[:, None] <= j[None, :]).astype(np.float32)
    sh["cmask"] = np.ascontiguousarray(np.stack([U, 1.0 - U, U.T.copy(), 1.0 - U.T, np.ones((128, 128), np.float32)]))
    sh["rrows"] = np.ascontiguousarray(np.stack([f(inp["r_k_k"])[:L], f(inp["r_k_a"])[:L], f(inp["r_r_k"])[:L].reshape(L, 1024),
                                                 f(inp["r_ln_g"])[:L], f(inp["r_ln_b"])[:L]], axis=1))
    sh["rmu"] = f(inp["r_mu"])[:L]
    sh["rw2"] = np.ascontiguousarray(np.concatenate([f(inp["r_w2"])[:L], f(inp["r_w0"])[:L][:, :, None, :]], axis=2))
    sh["ra2"] = np.ascontiguousarray(np.concatenate([f(inp["r_a2"])[:L], f(inp["r_a0"])[:L][:, :, None, :]], axis=2))
    sh["rg2"] = f(inp["r_g2"])[:L]
    sh["zrow"] = np.zeros((1, 3488), np.float32)
    sh["gprm"] = np.ascontiguousarray(np.concatenate([f(inp["g_dt_bias"])[:L].reshape(L, 16), f(inp["g_a_log"])[:L].reshape(L, 16),
                                                      f(inp["g_norm_g"])[:L]], axis=1))
    lm = np.zeros((2, 7, 128, 128), np.float32)
    for lev in range(7):
        sz = 1 << lev
        blk = j // (2 * sz)
        hi = (j % (2 * sz)) >= sz
        same = blk[:, None] == blk[None, :]
        lm[0, lev] = (same & (~hi)[:, None] & hi[None, :]).astype(np.float32)
        lm[1, lev] = (same & hi[:, None] & (~hi)[None, :]).astype(np.float32)
    sh["lmask"] = lm
    sh["mprm"] = np.ascontiguousarray(np.concatenate([f(inp["m_dt_bias"])[:L].reshape(L, 32), f(inp["m_a_log"])[:L].reshape(L, 32),
                                                      f(inp["m_d"])[:L]], axis=1))
    return sh


def prep_core(inp, b):
    f = lambda a: np.ascontiguousarray(np.asarray(a, dtype=np.float32))
    m = {}
    m["xin"] = np.concatenate([f(inp["ctx"])[b], f(inp["x"])[b]], axis=0)
    cc = np.stack([f(inp["c"])[b], f(inp["c_ctx"])], axis=1)
    m["cc"] = np.ascontiguousarray(cc.reshape(8, 128, 2).transpose(1, 0, 2))
    return m


NCORES = 2
GRID = 64


LAYERS_PER_LAUNCH = 4


def kernel(**inputs):
    L = 4
    n = LAYERS_PER_LAUNCH
    xin = [prep_core(inputs, b) for b in range(2)]
    out = None
    for l0 in range(0, L, n):
        mk = MK(GW=GRID, depth=n, debug_out=(["XR"] if l0 + n < L else []))
        nc = mk.build()
        sh = prep_shared(inputs, n, l0)
        in_maps = []
        for c in range(NCORES):
            m = dict(sh)
            m.update(xin[c % 2])
            in_maps.append(m)
        res = run_bass_kernel_spmd(nc, in_maps, core_ids=list(range(NCORES)))
        if l0 + n < L:
            for b in range(2):
                xin[b]["xin"] = np.ascontiguousarray(np.asarray(res.results[b]["XR"], dtype=np.float32))
        else:
            out = np.stack([np.asarray(res.results[b]["out"], dtype=np.float32) for b in range(2)], axis=0)
    return out
```
